# Optimizing a Trainium2 kernel written in Bass

```python
import jax, jax.numpy as jnp
from jax import lax
import numpy as np

D_MODEL = 2048
BATCH = 2
SEQ = 8192
DEPTH = 4

CHUNK = 64
GMLP_BLOCK = 128
D_A = 1024
A_GROUPS = 8
D_B = 1024
B_GROUPS = 8
CONV_B = 3
D_C = 1024
CONV_C = 31
N_BRANCH = 3
D_BR = 1024
D_FF = 5632
EPS = 1e-6
D_IN = 2 * D_A + 3 * D_B + 2 * D_C + N_BRANCH * D_MODEL

kernel_name = "hybrid_gmlp_shortconv_conformer_macaron"


def _split_points():
    widths = [D_A, D_A, D_B, D_B, D_B, D_C, D_C, N_BRANCH * D_MODEL]
    return [int(p) for p in np.cumsum(widths)[:-1]]


def rmsnorm(x, g):
    x32 = x.astype(jnp.float32)
    y = x32 * lax.rsqrt(jnp.mean(x32 * x32, axis=-1, keepdims=True) + EPS)
    return (y * g.astype(jnp.float32)).astype(x.dtype)


def layernorm(x, g, b):
    x32 = x.astype(jnp.float32)
    mu = jnp.mean(x32, axis=-1, keepdims=True)
    xc = x32 - mu
    y = xc * lax.rsqrt(jnp.mean(xc * xc, axis=-1, keepdims=True) + EPS)
    return (y * g.astype(jnp.float32) + b.astype(jnp.float32)).astype(x.dtype)


def swiglu(h, w1, w3, w2):
    return (jax.nn.silu(h @ w1) * (h @ w3)) @ w2


def causal_depthwise_conv(x, w):
    K, C = w.shape
    return lax.conv_general_dilated(
        x, w[:, None, :].astype(x.dtype), window_strides=(1,), padding=[(K - 1, 0)],
        dimension_numbers=("NWC", "WIO", "NWC"), feature_group_count=C)


def gmlp_spatial_gating(u, v, w_s, b_s, ln_g, ln_b):
    bsz, s, _ = v.shape
    v = layernorm(v, ln_g, ln_b)
    vb = v.reshape(bsz, s // GMLP_BLOCK, GMLP_BLOCK, A_GROUPS, D_A // A_GROUPS)
    chunk_id = jnp.arange(GMLP_BLOCK) // CHUNK
    mask = chunk_id[None, :] <= chunk_id[:, None]
    ws = jnp.where(mask[None], w_s, jnp.zeros_like(w_s))
    mixed = jnp.einsum("gij,bnjgc->bnigc", ws, vb) + b_s.T[None, None, :, :, None]
    return u * mixed.reshape(bsz, s, D_A)


def setup_inputs(seed: int = 0) -> dict:
    key = jax.random.key(seed)
    ks = jax.random.split(key, 24)
    f32 = jnp.float32

    def nrm(k, shape, scale):
        return jax.random.normal(k, shape, f32) * scale

    def gain(k, shape):
        return 1.0 + 0.05 * jax.random.normal(k, shape, f32)

    return {
        "x": jax.random.normal(ks[0], (BATCH, SEQ, D_MODEL), f32),
        "ffn1_norm": gain(ks[1], (DEPTH, D_MODEL)),
        "ffn1_w1": nrm(ks[2], (DEPTH, D_MODEL, D_FF), D_MODEL ** -0.5),
        "ffn1_w3": nrm(ks[3], (DEPTH, D_MODEL, D_FF), D_MODEL ** -0.5),
        "ffn1_w2": nrm(ks[4], (DEPTH, D_FF, D_MODEL), D_FF ** -0.5),
        "mix_norm": gain(ks[5], (DEPTH, D_MODEL)),
        "w_in": nrm(ks[6], (DEPTH, D_MODEL, D_IN), D_MODEL ** -0.5),
        "b_gate": nrm(ks[7], (DEPTH, N_BRANCH, D_MODEL), 0.02),
        "gmlp_ln_g": gain(ks[8], (DEPTH, D_A)),
        "gmlp_ln_b": nrm(ks[9], (DEPTH, D_A), 0.02),
        "gmlp_w_s": nrm(ks[10], (DEPTH, A_GROUPS, GMLP_BLOCK, GMLP_BLOCK), GMLP_BLOCK ** -0.5),
        "gmlp_b_s": gain(ks[11], (DEPTH, A_GROUPS, GMLP_BLOCK)),
        "sconv_w": nrm(ks[12], (DEPTH, CONV_B, D_B), CONV_B ** -0.5),
        "conf_conv_w": nrm(ks[13], (DEPTH, CONV_C, D_C), CONV_C ** -0.5),
        "conf_conv_b": nrm(ks[14], (DEPTH, D_C), 0.02),
        "conf_ln_g": gain(ks[15], (DEPTH, D_C)),
        "conf_ln_b": nrm(ks[16], (DEPTH, D_C), 0.02),
        "w_branch": nrm(ks[17], (DEPTH, N_BRANCH, D_BR, D_MODEL), D_BR ** -0.5),
        "w_out": nrm(ks[18], (DEPTH, D_MODEL, D_MODEL), D_MODEL ** -0.5),
        "ffn2_norm": gain(ks[19], (DEPTH, D_MODEL)),
        "ffn2_w1": nrm(ks[20], (DEPTH, D_MODEL, D_FF), D_MODEL ** -0.5),
        "ffn2_w3": nrm(ks[21], (DEPTH, D_MODEL, D_FF), D_MODEL ** -0.5),
        "ffn2_w2": nrm(ks[22], (DEPTH, D_FF, D_MODEL), D_FF ** -0.5),
        "final_norm": gain(ks[23], (D_MODEL,)),
    }


def reference(x, ffn1_norm, ffn1_w1, ffn1_w3, ffn1_w2, mix_norm, w_in, b_gate,
              gmlp_ln_g, gmlp_ln_b, gmlp_w_s, gmlp_b_s, sconv_w, conf_conv_w,
              conf_conv_b, conf_ln_g, conf_ln_b, w_branch, w_out, ffn2_norm,
              ffn2_w1, ffn2_w3, ffn2_w2, final_norm):
    bsz, s, _ = x.shape
    splits = _split_points()
    for l in range(DEPTH):
        x = x + 0.5 * swiglu(rmsnorm(x, ffn1_norm[l]), ffn1_w1[l], ffn1_w3[l], ffn1_w2[l])

        h = rmsnorm(x, mix_norm[l])
        proj = h @ w_in[l]
        u, v, b_g, c_g, x_b, glu_val, glu_gate, gates = jnp.split(proj, splits, axis=-1)

        y_a = gmlp_spatial_gating(jax.nn.gelu(u, approximate=False),
                                  jax.nn.gelu(v, approximate=False),
                                  gmlp_w_s[l], gmlp_b_s[l], gmlp_ln_g[l], gmlp_ln_b[l])

        y_b = b_g * causal_depthwise_conv(c_g * x_b, sconv_w[l])

        z = glu_val * jax.nn.sigmoid(glu_gate)
        z = causal_depthwise_conv(z, conf_conv_w[l]) + conf_conv_b[l]
        y_c = jax.nn.silu(layernorm(z, conf_ln_g[l], conf_ln_b[l]))

        gates = jax.nn.sigmoid(gates.reshape(bsz, s, N_BRANCH, D_MODEL) + b_gate[l])
        merged = (gates[:, :, 0] * (y_a @ w_branch[l, 0])
                  + gates[:, :, 1] * (y_b @ w_branch[l, 1])
                  + gates[:, :, 2] * (y_c @ w_branch[l, 2]))
        x = x + merged @ w_out[l]

        x = x + 0.5 * swiglu(rmsnorm(x, ffn2_norm[l]), ffn2_w1[l], ffn2_w3[l], ffn2_w2[l])
    return rmsnorm(x, final_norm)
```

```python
import numpy as np
import concourse.bass as bass
import concourse.mybir as mybir
from concourse.bass_utils import run_bass_kernel_spmd

F32 = mybir.dt.float32
BF16 = mybir.dt.bfloat16
AF = mybir.ActivationFunctionType
ALU = mybir.AluOpType

D = 2048
KD = 16
DFF = 5632
NFC = 44
NGRP = 11
HALO = 128
TOUT = 1024
T = HALO + TOUT
NT = 3
TT = 384
NBLK = 9
L = 4
NSLOT = 8
NCORES = 8
EPS = 1e-6

O_F1N, O_MN, O_F2N, O_BG, O_SCW, O_CCW, O_CCB, O_CLG, O_CLB = 0, 16, 32, 48, 96, 120, 368, 376, 384
NS = 392


class Buf:
    __slots__ = ("w", "r")

    def __init__(self):
        self.w = None
        self.r = {}

    def toks(self):
        out = list(self.r.values())
        if self.w is not None:
            out.append(self.w)
        return out


class Prog:
    ENG = ("pe", "act", "dve", "pool", "sp")

    def __init__(self, nc, sems):
        self.nc = nc
        self.q = {e: [] for e in self.ENG}
        self.sem = sems
        self.cnt = {e: 0 for e in self.ENG}
        self.seen = {e: {} for e in self.ENG}
        self.dcnt = {}

    def wait(self, eng, tok):
        if tok is None:
            return
        sem, val = tok
        key = id(sem)
        if self.seen[eng].get(key, 0) >= val:
            return
        self.seen[eng][key] = val
        self.q[eng].append(("w", sem, val))

    def _deps(self, eng, reads, writes, extra):
        for b in reads:
            self.wait(eng, b.w)
        for b in writes:
            self.wait(eng, b.w)
            for t in b.r.values():
                self.wait(eng, t)
        for t in extra:
            self.wait(eng, t)

    def _note(self, tok, reads, writes):
        key = id(tok[0])
        for b in reads:
            old = b.r.get(key)
            if old is None or old[1] < tok[1]:
                b.r[key] = tok
        for b in writes:
            b.w = tok
            b.r = {}

    def op(self, eng, fn, reads=(), writes=(), extra=()):
        self._deps(eng, reads, writes, extra)
        self.cnt[eng] += 1
        tok = (self.sem[eng], self.cnt[eng])
        self.q[eng].append(("i", fn, 1))
        self._note(tok, reads, writes)
        return tok

    def mm(self, groups, bank, reads):
        self._deps("pe", reads, (bank,), ())
        flat = []
        for out_ap, mms in groups:
            n = len(mms)
            for i, (lt, rh) in enumerate(mms):
                flat.append((out_ap, lt, rh, i == 0, i == n - 1))
        self.cnt["pe"] += 1
        tok = (self.sem["pe"], self.cnt["pe"])
        nc = self.nc
        for idx, (o, lt, rh, st, sp) in enumerate(flat):
            last = idx == len(flat) - 1
            self.q["pe"].append(("i", (lambda o=o, lt=lt, rh=rh, st=st, sp=sp:
                                       nc.tensor.matmul(o, lhsT=lt, rhs=rh, start=st, stop=sp)), 1 if last else 0))
        self._note(tok, reads, (bank,))
        return tok

    def dma(self, eng, fn, sem, reads=(), writes=(), extra=()):
        self._deps(eng, reads, writes, extra)
        self.dcnt[id(sem)] = self.dcnt.get(id(sem), 0) + 16
        tok = (sem, self.dcnt[id(sem)])
        self.q[eng].append(("d", fn, sem))
        self._note(tok, reads, writes)
        return tok

    def replay(self, eng, engobj):
        sem = self.sem[eng]
        for item in self.q[eng]:
            if item[0] == "w":
                engobj.wait_ge(item[1], item[2])
            elif item[0] == "i":
                ins = item[1]()
                if item[2]:
                    ins.then_inc(sem, 1)
            else:
                item[1]().then_inc(item[2], 16)


def handoff(old_bufs, new_bufs):
    merged = {}
    for b in old_bufs:
        for t in b.toks():
            k = id(t[0])
            if k not in merged or merged[k][1] < t[1]:
                merged[k] = t
    for nb in new_bufs:
        for k, t in merged.items():
            o = nb.r.get(k)
            if o is None or o[1] < t[1]:
                nb.r[k] = t


def build_program(n_layers=L, n_passes=2):
    nc = bass.Bass("TRN2", target_bir_lowering=False)

    def din(name, shape):
        return nc.dram_tensor(name, shape, F32, kind="ExternalInput").ap()

    xin = din("xin", [n_passes, 128, KD * T])
    maskd = din("mask", [n_passes, 128, 1])
    fw1 = [din("f1w1", [L, NFC, 128, 2048]), din("f2w1", [L, NFC, 128, 2048])]
    fw3 = [din("f1w3", [L, NFC, 128, 2048]), din("f2w3", [L, NFC, 128, 2048])]
    fw2 = [din("f1w2", [L, DFF, D]), din("f2w2", [L, DFF, D])]
    win = din("win", [L, 104, 128, 2048])
    wbr = din("wbr", [L, 48, 128, 1024])
    wout = din("wout", [L, 16, 128, 2048])
    smalld = din("small", [128, L * NS])
    bcd = din("bc", [L, 128, 4096])
    find = din("fin", [128, 16])
    identd = din("ident", [128, 128])
    xsp = nc.dram_tensor("xspill", [KD, 128, T], F32, kind="Internal").ap()
    outd = nc.dram_tensor("out", [n_passes, KD, 128, TOUT], F32, kind="ExternalOutput").ap()

    W_X, W_H, W_M = KD * T, KD * T // 2, KD * T // 2
    W_RING = NSLOT * 1024
    sizes = dict(X=W_X, H=W_H, M=W_M, RING=W_RING, SMALL=L * NS, FIN=16, WST=512, RSTD=T, ONES=64, IDENT=64,
                 MASK=2, SQT=2 * TT, SA=2 * TT, STA=NBLK * 12, MVA=NBLK * 2, RSA=NBLK * 2)
    NBIG = sum(sizes.values())
    assert NBIG * 4 <= 207 * 1024, NBIG * 4

    import contextlib
    es = contextlib.ExitStack()
    big = es.enter_context(nc.sbuf_tensor("big", [128, NBIG], F32))
    ps = [es.enter_context(nc.psum_tensor(f"ps{i}", [128, 512], F32)) for i in range(8)]
    esem = {e: es.enter_context(nc.semaphore(f"s_{e}")) for e in Prog.ENG}
    ring_sem = [es.enter_context(nc.semaphore(f"ring{i}")) for i in range(NSLOT)]
    xk_sem = [es.enter_context(nc.semaphore(f"xk{i}")) for i in range(KD)]
    out_sem = [es.enter_context(nc.semaphore(f"o{i}")) for i in range(2)]
    setup_sem = es.enter_context(nc.semaphore("setup"))
    bc_sem = es.enter_context(nc.semaphore("bcs"))
    mask_sem = es.enter_context(nc.semaphore("masks"))

    off = {}
    o = 0
    for k_, v_ in sizes.items():
        off[k_] = o
        o += v_

    def reg(name, a=0, n=None):
        n = sizes[name] - a if n is None else n
        return big[:, off[name] + a: off[name] + a + n]

    Xf = reg("X").rearrange("p (k t) -> p k t", k=KD)
    Hb = reg("H").bitcast(BF16).rearrange("p (k t) -> p k t", k=KD)
    Mreg = reg("M")
    ringv = [reg("RING", s * 1024, 1024).bitcast(BF16) for s in range(NSLOT)]
    SMALL = reg("SMALL")
    FIN = reg("FIN")
    WST = reg("WST").bitcast(BF16)
    RSTD = reg("RSTD")
    ONES = reg("ONES").bitcast(BF16)
    IDENT = reg("IDENT").bitcast(BF16)
    MASK = reg("MASK")
    SQT = [reg("SQT", i * TT, TT) for i in range(2)]
    SA = [reg("SA", i * TT, TT) for i in range(2)]
    STA = reg("STA").rearrange("p (n s) -> p n s", n=NBLK)
    MVA = reg("MVA").rearrange("p (n s) -> p n s", n=NBLK)
    RSA = reg("RSA")

    SLOTW = 4 * T

    def xslot(i, a=0, n=SLOTW):
        return big[:, off["X"] + i * SLOTW + a: off["X"] + i * SLOTW + a + n]

    YA = xslot(0).bitcast(BF16).rearrange("p (k t) -> p k t", k=8)
    YB = xslot(1).bitcast(BF16).rearrange("p (k t) -> p k t", k=8)
    YC = xslot(2).bitcast(BF16).rearrange("p (k t) -> p k t", k=8)
    BCT = xslot(2, 0, 4096)
    TMPS = xslot(1, 0, 512)
    VLN = xslot(3).bitcast(BF16).rearrange("p (n f) -> p n f", n=NBLK)
    SGC = xslot(3, 0, T)
    ZB = [xslot(3, T + i * 592, 592).bitcast(BF16) for i in range(2)]
    DIAG = xslot(3, T + 1184, 1984).bitcast(BF16)
    CG = xslot(3, 0, T)
    SS = xslot(3, T, T + 2)
    CS = xslot(3, 2 * T + 2, T)
    ZCB = xslot(3, 0, 1536).bitcast(BF16).rearrange("p (j t) -> p j t", j=8)
    ZSQ = xslot(3, 1536, 1536).bitcast(BF16).rearrange("p (j t) -> p j t", j=8)
    LT = [xslot(3, 3072 + i * TT, TT) for i in range(4)]
    SGM = [xslot(3, i * 576, 576).bitcast(BF16) for i in range(2)]
    MF = xslot(3, 1152, T)
    MT = xslot(3, 1152 + T, T)
    GR = Mreg.bitcast(BF16).rearrange("p (s c t) -> p s c t", s=2, c=4)
    XSQ = Mreg.bitcast(BF16).rearrange("p (k t) -> p k t", k=KD)
    VG = Mreg.rearrange("p (n f) -> p n f", n=NBLK)
    ZC = Mreg.rearrange("p (j t) -> p j t", j=8)
    MG = Mreg.bitcast(BF16).rearrange("p (k t) -> p k t", k=KD)
    OUTR = [big[:, off["H"] + i * TOUT: off["H"] + (i + 1) * TOUT] for i in range(2)]

    P = Prog(nc, esem)

    xB = [[Buf() for _ in range(NT)] for _ in range(KD)]
    hB = [[Buf() for _ in range(NT)] for _ in range(KD)]
    xspB = [Buf() for _ in range(KD)]
    bankB = [Buf() for _ in range(8)]
    slotB = [Buf() for _ in range(NSLOT)]
    rstdB = [Buf() for _ in range(NT)]
    sqtB = [Buf() for _ in range(2)]
    saB = [Buf() for _ in range(2)]
    constB = Buf()
    epsB = Buf()
    maskB = Buf()
    wstB = Buf()
    statB = Buf()

    st = dict(bank=0, piece=0, sa=0, sq=0)

    def next_bank():
        b = st["bank"] % 8
        st["bank"] += 1
        return b

    def fetch(dram2d, n):
        s = st["piece"] % NSLOT
        st["piece"] += 1
        dst = ringv[s][:, 0:n]
        P.dma("pool", (lambda dst=dst, src=dram2d: nc.gpsimd.dma_start(out=dst, in_=src)),
              ring_sem[s], writes=(slotB[s],))
        return ringv[s], slotB[s]

    def tsl(tt):
        return slice(tt * TT, (tt + 1) * TT)

    tsetup = []
    tsetup.append(P.dma("sp", lambda: nc.sync.dma_start(out=SMALL, in_=smalld), setup_sem, writes=(constB,)))
    tsetup.append(P.dma("sp", lambda: nc.sync.dma_start(out=FIN, in_=find), setup_sem, writes=(constB,)))
    P.dma("pool", lambda: nc.gpsimd.dma_start(out=IDENT, in_=identd), setup_sem, writes=(constB,))
    constB.w = (setup_sem, 48)
    onesB = Buf()
    P.op("dve", lambda: nc.vector.memset(ONES, 1.0), writes=(onesB,))

    def scol(l, o_, n=1):
        return SMALL[:, l * NS + o_: l * NS + o_ + n]

    def norm_stats(m_old_bufs):
        xsqB = [[Buf() for _ in range(NT)] for _ in range(KD)]
        handoff(m_old_bufs, [b for r_ in xsqB for b in r_])
        for tt in range(NT):
            for k in range(KD):
                P.op("act", (lambda k=k, tt=tt: nc.scalar.activation(out=XSQ[:, k, tsl(tt)], in_=Xf[:, k, tsl(tt)],
                                                                    func=AF.Square)),
                     reads=(xB[k][tt],), writes=(xsqB[k][tt],))
        for tt in range(NT):
            b = next_bank()
            P.mm([(ps[b][:, 0:TT], [(ONES, XSQ[:, k, tsl(tt)]) for k in range(KD)])], bankB[b],
                 reads=[xsqB[k][tt] for k in range(KD)] + [onesB])
            q = st["sq"] % 2
            st["sq"] += 1
            P.op("act", (lambda b=b, q=q: nc.scalar.activation(out=SQT[q], in_=ps[b][:, 0:TT], func=AF.Sqrt,
                                                               bias=EPSC, scale=1.0 / D)),
                 reads=(bankB[b], epsB), writes=(sqtB[q],))
            P.op("dve", (lambda q=q, tt=tt: nc.vector.reciprocal(out=RSTD[:, tsl(tt)], in_=SQT[q])),
                 reads=(sqtB[q],), writes=(rstdB[tt],))
        return [b for r_ in xsqB for b in r_]

    def norm_h(l, ocol, m_old_bufs):
        xs = norm_stats(m_old_bufs)
        for tt in range(NT):
            for k in range(KD):
                P.op("dve", (lambda k=k, tt=tt: nc.vector.scalar_tensor_tensor(
                    out=Hb[:, k, tsl(tt)], in0=Xf[:, k, tsl(tt)], scalar=scol(l, ocol + k),
                    in1=RSTD[:, tsl(tt)], op0=ALU.mult, op1=ALU.mult)),
                    reads=(xB[k][tt], rstdB[tt], constB), writes=(hB[k][tt],))
        return xs

    def ffn(l, which, m_old_bufs):
        ocol = O_F1N if which == 0 else O_F2N
        xs = norm_h(l, ocol, m_old_bufs)
        gB = [[[Buf() for _ in range(NT)] for _ in range(4)] for _ in range(2)]
        handoff(xs, [b for s_ in gB for c_ in s_ for b in c_])
        w1, w3, w2 = fw1[which], fw3[which], fw2[which]

        def p1(G):
            s = G % 2
            for ci in range(4):
                c = G * 4 + ci
                pa, pab = fetch(w1[l, c], 2048)
                pb, pbb = fetch(w3[l, c], 2048)
                for tt in range(NT):
                    ba = next_bank()
                    P.mm([(ps[ba][:, 0:TT], [(pa[:, k * 128:(k + 1) * 128], Hb[:, k, tsl(tt)]) for k in range(KD)])],
                         bankB[ba], reads=[pab] + [hB[k][tt] for k in range(KD)])
                    bb = next_bank()
                    P.mm([(ps[bb][:, 0:TT], [(pb[:, k * 128:(k + 1) * 128], Hb[:, k, tsl(tt)]) for k in range(KD)])],
                         bankB[bb], reads=[pbb] + [hB[k][tt] for k in range(KD)])
                    q = st["sa"] % 2
                    st["sa"] += 1
                    P.op("act", (lambda ba=ba, q=q: nc.scalar.activation(out=SA[q], in_=ps[ba][:, 0:TT], func=AF.Silu)),
                         reads=(bankB[ba],), writes=(saB[q],))
                    P.op("dve", (lambda bb=bb, q=q, s=s, ci=ci, tt=tt: nc.vector.tensor_tensor(
                        out=GR[:, s, ci, tsl(tt)], in0=ps[bb][:, 0:TT], in1=SA[q], op=ALU.mult)),
                        reads=(bankB[bb], saB[q]), writes=(gB[s][ci][tt],))

        def p2(G):
            s = G % 2
            pw = [fetch(w2[l, (G * 4 + ci) * 128:(G * 4 + ci + 1) * 128, :], 2048) for ci in range(4)]
            for j in range(KD):
                for tt in range(NT):
                    b = next_bank()
                    P.mm([(ps[b][:, 0:TT], [(pw[ci][0][:, j * 128:(j + 1) * 128], GR[:, s, ci, tsl(tt)])
                                            for ci in range(4)])],
                         bankB[b], reads=[pw[ci][1] for ci in range(4)] + [gB[s][ci][tt] for ci in range(4)])
                    P.op("dve", (lambda b=b, j=j, tt=tt: nc.vector.scalar_tensor_tensor(
                        out=Xf[:, j, tsl(tt)], in0=ps[b][:, 0:TT], scalar=0.5, in1=Xf[:, j, tsl(tt)],
                        op0=ALU.mult, op1=ALU.add)),
                        reads=(bankB[b],), writes=(xB[j][tt],))

        for G in range(NGRP + 1):
            if G < NGRP:
                p1(G)
            if G >= 1:
                p2(G - 1)
        return [b for s_ in gB for c_ in s_ for b in c_]

    def mixer(l, m_old_bufs):
        xs = norm_h(l, O_MN, m_old_bufs)
        for k in range(KD):
            P.dma("sp", (lambda k=k: nc.sync.dma_start(out=xsp[k], in_=Xf[:, k, :])), xk_sem[k],
                  reads=tuple(xB[k]), writes=(xspB[k],))
        yaB = [[Buf() for _ in range(NT)] for _ in range(8)]
        ybB = [[Buf() for _ in range(NT)] for _ in range(8)]
        ycB = [[Buf() for _ in range(NT)] for _ in range(8)]
        bcB, tmpsB = Buf(), Buf()
        vlnB = [Buf() for _ in range(NBLK)]
        allx = [b for r_ in xB for b in r_]
        handoff(allx, [b for r_ in yaB for b in r_] + [bcB] + vlnB + [tmpsB])
        P.dma("sp", (lambda: nc.sync.dma_start(out=BCT, in_=bcd[l])), bc_sem, writes=(bcB,))
        P.op("dve", lambda: nc.vector.tensor_copy(out=WST, in_=BCT[:, 3072:4096]), reads=(bcB,), writes=(wstB,))
        wst3 = WST.rearrange("p (g i) -> p g i", g=8)
        P.op("dve", lambda: nc.vector.memset(wst3[64:128, :, 0:64], 0.0), writes=(wstB,))

        vgB = [[Buf() for _ in range(8)] for _ in range(3)]
        handoff(xs, [b for r_ in vgB for b in r_])
        for c in range(8):
            pv, pvb = fetch(win[l, 8 + c], 2048)
            for bg in range(3):
                b = next_bank()
                groups = []
                for bi in range(3):
                    n = bg * 3 + bi
                    groups.append((ps[b][:, bi * 128:(bi + 1) * 128],
                                   [(Hb[:, k, n * 128:(n + 1) * 128], pv[:, k * 128:(k + 1) * 128]) for k in range(KD)]))
                P.mm(groups, bankB[b], reads=[pvb] + [hB[k][bg] for k in range(KD)])
                P.op("act", (lambda b=b, bg=bg, c=c: nc.scalar.activation(
                    out=VG[:, bg * 3:(bg + 1) * 3, c * 128:(c + 1) * 128],
                    in_=ps[b][:, 0:384].rearrange("p (n f) -> p n f", n=3), func=AF.Gelu)),
                    reads=(bankB[b],), writes=(vgB[bg][c],))
        for n in range(NBLK):
            rd = tuple(vgB[n // 3])
            P.op("dve", (lambda n=n: nc.vector.bn_stats(out=STA[:, n, 0:6], in_=VG[:, n, 0:512])), reads=rd, writes=(statB,))
            P.op("dve", (lambda n=n: nc.vector.bn_stats(out=STA[:, n, 6:12], in_=VG[:, n, 512:1024])), reads=rd, writes=(statB,))
            P.op("dve", (lambda n=n: nc.vector.bn_aggr(out=MVA[:, n, :], in_=STA[:, n, :])), reads=(statB,), writes=(statB,))
        P.op("act", (lambda: nc.scalar.activation(out=RSA[:, 0:NBLK], in_=MVA[:, :, 1], func=AF.Sqrt, bias=EPSC, scale=1.0)),
             reads=(statB, epsB), writes=(statB,))
        P.op("dve", (lambda: nc.vector.reciprocal(out=RSA[:, NBLK:2 * NBLK], in_=RSA[:, 0:NBLK])), reads=(statB,), writes=(statB,))
        for c in range(8):
            pu, pub = fetch(win[l, c], 2048)
            for tt in range(NT):
                b = next_bank()
                P.mm([(ps[b][:, 0:TT], [(pu[:, k * 128:(k + 1) * 128], Hb[:, k, tsl(tt)]) for k in range(KD)])],
                     bankB[b], reads=[pub] + [hB[k][tt] for k in range(KD)])
                P.op("act", (lambda b=b, c=c, tt=tt: nc.scalar.activation(out=YA[:, c, tsl(tt)], in_=ps[b][:, 0:TT], func=AF.Gelu)),
                     reads=(bankB[b],), writes=(yaB[c][tt],))
        for n in range(NBLK):
            rd = tuple(vgB[n // 3])
            P.op("dve", (lambda n=n: nc.vector.tensor_scalar(out=VG[:, n, :], in0=VG[:, n, :], scalar1=MVA[:, n, 0:1],
                                                             scalar2=RSA[:, NBLK + n:NBLK + n + 1],
                                                             op0=ALU.subtract, op1=ALU.mult)),
                 reads=(statB,), writes=rd)
            P.op("dve", (lambda n=n: nc.vector.tensor_tensor(out=VG[:, n, :], in0=VG[:, n, :], in1=BCT[:, 0:1024], op=ALU.mult)),
                 reads=(bcB,), writes=rd)
            P.op("dve", (lambda n=n: nc.vector.tensor_tensor(out=VLN[:, n, :], in0=VG[:, n, :], in1=BCT[:, 1024:2048], op=ALU.add)),
                 reads=rd + (bcB,), writes=(vlnB[n],))
        for n in range(NBLK):
            tt = n // 3
            for half in range(2):
                b = next_bank()
                groups = []
                for gi in range(4):
                    g = half * 4 + gi
                    groups.append((ps[b][:, gi * 128:(gi + 1) * 128],
                                   [(VLN[:, n, g * 128:(g + 1) * 128], WST[:, g * 128:(g + 1) * 128])]))
                P.mm(groups, bankB[b], reads=[vlnB[n], wstB])
                P.op("dve", (lambda b=b, half=half: nc.vector.tensor_tensor(
                    out=TMPS, in0=ps[b][:, 0:512], in1=BCT[:, 2048 + half * 512: 2048 + (half + 1) * 512], op=ALU.add)),
                    reads=(bankB[b], bcB), writes=(tmpsB,))
                P.op("dve", (lambda half=half, n=n: nc.vector.tensor_tensor(
                    out=YA[:, half * 4:(half + 1) * 4, n * 128:(n + 1) * 128],
                    in0=TMPS.rearrange("p (g i) -> p g i", g=4),
                    in1=YA[:, half * 4:(half + 1) * 4, n * 128:(n + 1) * 128], op=ALU.mult)),
                    reads=(tmpsB,), writes=tuple(yaB[half * 4 + gi][tt] for gi in range(4)))

        zcB = [[Buf() for _ in range(NT)] for _ in range(8)]
        handoff([b for r_ in vgB for b in r_], [b for r_ in zcB for b in r_])
        sgB = [Buf() for _ in range(NT)]
        zbB = [Buf() for _ in range(2)]
        diagB = Buf()
        handoff(vlnB, sgB + zbB + [diagB])
        pend = None

        def conv_c(j, r):
            for tt in range(NT):
                b = next_bank()
                P.mm([(ps[b][:, 0:TT], [(DIAG[:, kk * 128:(kk + 1) * 128], ZB[r][:, kk + tt * TT: kk + tt * TT + TT])
                                        for kk in range(31)])], bankB[b], reads=[diagB, zbB[r]])
                P.op("act", (lambda b=b, j=j, tt=tt: nc.scalar.activation(
                    out=ZC[:, j, tsl(tt)], in_=ps[b][:, 0:TT], func=AF.Identity, bias=scol(l, O_CCB + j), scale=1.0)),
                    reads=(bankB[b], constB), writes=(zcB[j][tt],))

        for j in range(8):
            r = j % 2
            pg, pgb = fetch(win[l, 48 + j], 2048)
            pvv, pvvb = fetch(win[l, 40 + j], 2048)
            for tt in range(NT):
                b = next_bank()
                P.mm([(ps[b][:, 0:TT], [(pg[:, k * 128:(k + 1) * 128], Hb[:, k, tsl(tt)]) for k in range(KD)])],
                     bankB[b], reads=[pgb] + [hB[k][tt] for k in range(KD)])
                P.op("act", (lambda b=b, tt=tt: nc.scalar.activation(out=SGC[:, tsl(tt)], in_=ps[b][:, 0:TT], func=AF.Sigmoid)),
                     reads=(bankB[b],), writes=(sgB[tt],))
            P.op("dve", (lambda r=r: nc.vector.memset(ZB[r][:, 0:30], 0.0)), writes=(zbB[r],))
            for tt in range(NT):
                b = next_bank()
                P.mm([(ps[b][:, 0:TT], [(pvv[:, k * 128:(k + 1) * 128], Hb[:, k, tsl(tt)]) for k in range(KD)])],
                     bankB[b], reads=[pvvb] + [hB[k][tt] for k in range(KD)])
                P.op("dve", (lambda b=b, tt=tt, r=r: nc.vector.tensor_tensor(
                    out=ZB[r][:, 30 + tt * TT: 30 + (tt + 1) * TT], in0=ps[b][:, 0:TT], in1=SGC[:, tsl(tt)], op=ALU.mult)),
                    reads=(bankB[b], sgB[tt]), writes=(zbB[r],))
            P.op("dve", (lambda r=r: nc.vector.tensor_scalar(out=ZB[r][:, 30:30 + HALO], in0=ZB[r][:, 30:30 + HALO],
                                                             scalar1=MASK[:, 0:1], scalar2=None, op0=ALU.mult)),
                 reads=(maskB,), writes=(zbB[r],))
            if pend is not None:
                conv_c(*pend)
            for kk in range(31):
                P.op("act", (lambda kk=kk, j=j: nc.scalar.activation(out=DIAG[:, kk * 128:(kk + 1) * 128], in_=IDENT, func=AF.Copy,
                                                                    scale=scol(l, O_CCW + j * 31 + kk))),
                     reads=(constB,), writes=(diagB,))
            pend = (j, r)
        conv_c(*pend)

        cgB = [Buf() for _ in range(NT)]
        ssB, csB = Buf(), Buf()
        handoff(sgB + zbB + [diagB], cgB + [ssB, csB])
        handoff([tmpsB], [b for r_ in ybB for b in r_])
        for j in range(8):
            pc, pcb = fetch(win[l, 24 + j], 2048)
            px, pxb = fetch(win[l, 32 + j], 2048)
            pbg, pbgb = fetch(win[l, 16 + j], 2048)
            for tt in range(NT):
                b = next_bank()
                P.mm([(ps[b][:, 0:TT], [(pc[:, k * 128:(k + 1) * 128], Hb[:, k, tsl(tt)]) for k in range(KD)])],
                     bankB[b], reads=[pcb] + [hB[k][tt] for k in range(KD)])
                P.op("act", (lambda b=b, tt=tt: nc.scalar.activation(out=CG[:, tsl(tt)], in_=ps[b][:, 0:TT], func=AF.Copy)),
                     reads=(bankB[b],), writes=(cgB[tt],))
            P.op("dve", (lambda: nc.vector.memset(SS[:, 0:2], 0.0)), writes=(ssB,))
            for tt in range(NT):
                b = next_bank()
                P.mm([(ps[b][:, 0:TT], [(px[:, k * 128:(k + 1) * 128], Hb[:, k, tsl(tt)]) for k in range(KD)])],
                     bankB[b], reads=[pxb] + [hB[k][tt] for k in range(KD)])
                P.op("dve", (lambda b=b, tt=tt: nc.vector.tensor_tensor(
                    out=SS[:, 2 + tt * TT: 2 + (tt + 1) * TT], in0=ps[b][:, 0:TT], in1=CG[:, tsl(tt)], op=ALU.mult)),
                    reads=(bankB[b], cgB[tt]), writes=(ssB,))
            P.op("dve", (lambda: nc.vector.tensor_scalar(out=SS[:, 2:2 + HALO], in0=SS[:, 2:2 + HALO], scalar1=MASK[:, 0:1],
                                                         scalar2=None, op0=ALU.mult)), reads=(maskB,), writes=(ssB,))
            P.op("dve", (lambda j=j: nc.vector.tensor_scalar(out=CS, in0=SS[:, 0:T], scalar1=scol(l, O_SCW + j * 3 + 0),
                                                             scalar2=None, op0=ALU.mult)), reads=(ssB, constB), writes=(csB,))
            for kk in (1, 2):
                P.op("dve", (lambda j=j, kk=kk: nc.vector.scalar_tensor_tensor(
                    out=CS, in0=SS[:, kk:kk + T], scalar=scol(l, O_SCW + j * 3 + kk), in1=CS, op0=ALU.mult, op1=ALU.add)),
                    reads=(ssB, constB), writes=(csB,))
            for tt in range(NT):
                b = next_bank()
                P.mm([(ps[b][:, 0:TT], [(pbg[:, k * 128:(k + 1) * 128], Hb[:, k, tsl(tt)]) for k in range(KD)])],
                     bankB[b], reads=[pbgb] + [hB[k][tt] for k in range(KD)])
                P.op("dve", (lambda b=b, tt=tt, j=j: nc.vector.tensor_tensor(
                    out=YB[:, j, tsl(tt)], in0=ps[b][:, 0:TT], in1=CS[:, tsl(tt)], op=ALU.mult)),
                    reads=(bankB[b], csB), writes=(ybB[j][tt],))

        zcbB, zsqB = Buf(), Buf()
        ltB = [Buf() for _ in range(4)]
        handoff(cgB + [ssB, csB], [zcbB, zsqB] + ltB)
        handoff([bcB], [b for r_ in ycB for b in r_])
        for tt in range(NT):
            for j in range(8):
                P.op("act", (lambda j=j, tt=tt: nc.scalar.activation(out=ZCB[:, j, :], in_=ZC[:, j, tsl(tt)], func=AF.Copy)),
                     reads=(zcB[j][tt],), writes=(zcbB,))
                P.op("act", (lambda j=j, tt=tt: nc.scalar.activation(out=ZSQ[:, j, :], in_=ZC[:, j, tsl(tt)], func=AF.Square)),
                     reads=(zcB[j][tt],), writes=(zsqB,))
            b1 = next_bank()
            P.mm([(ps[b1][:, 0:TT], [(ONES, ZCB[:, j, :]) for j in range(8)])], bankB[b1], reads=[zcbB, onesB])
            b2 = next_bank()
            P.mm([(ps[b2][:, 0:TT], [(ONES, ZSQ[:, j, :]) for j in range(8)])], bankB[b2], reads=[zsqB, onesB])
            P.op("act", (lambda b1=b1: nc.scalar.activation(out=LT[0], in_=ps[b1][:, 0:TT], func=AF.Square, scale=1.0 / 1024)),
                 reads=(bankB[b1],), writes=(ltB[0],))
            P.op("dve", (lambda b2=b2: nc.vector.scalar_tensor_tensor(out=LT[1], in0=ps[b2][:, 0:TT], scalar=1.0 / 1024,
                                                                      in1=LT[0], op0=ALU.mult, op1=ALU.subtract)),
                 reads=(bankB[b2], ltB[0]), writes=(ltB[1],))
            P.op("act", (lambda: nc.scalar.activation(out=LT[0], in_=LT[1], func=AF.Sqrt, bias=EPSC, scale=1.0)),
                 reads=(ltB[1], epsB), writes=(ltB[0],))
            P.op("dve", (lambda tt=tt: nc.vector.reciprocal(out=RSTD[:, tsl(tt)], in_=LT[0])), reads=(ltB[0],), writes=(rstdB[tt],))
            for j in range(8):
                P.op("dve", (lambda b1=b1, j=j, tt=tt: nc.vector.scalar_tensor_tensor(
                    out=LT[2], in0=ps[b1][:, 0:TT], scalar=-1.0 / 1024, in1=ZC[:, j, tsl(tt)], op0=ALU.mult, op1=ALU.add)),
                    reads=(bankB[b1], zcB[j][tt]), writes=(ltB[2],))
                P.op("dve", (lambda tt=tt: nc.vector.tensor_tensor(out=LT[3], in0=LT[2], in1=RSTD[:, tsl(tt)], op=ALU.mult)),
                     reads=(ltB[2], rstdB[tt]), writes=(ltB[3],))
                P.op("act", (lambda j=j, tt=tt: nc.scalar.activation(out=YC[:, j, tsl(tt)], in_=LT[3], func=AF.Silu,
                                                                    bias=scol(l, O_CLB + j), scale=scol(l, O_CLG + j))),
                     reads=(ltB[3], constB), writes=(ycB[j][tt],))

        mgB = [[Buf() for _ in range(NT)] for _ in range(KD)]
        handoff([b for r_ in zcB for b in r_], [b for r_ in mgB for b in r_])
        sgmB = [[Buf() for _ in range(NT)] for _ in range(2)]
        mfB = [Buf() for _ in range(NT)]
        mtB = [Buf() for _ in range(NT)]
        handoff([zcbB, zsqB] + ltB, [b for r_ in sgmB for b in r_] + mfB + mtB)
        ysrc = [(YA, yaB), (YB, ybB), (YC, ycB)]
        cnt = 0
        for j in range(KD):
            for i in range(3):
                r = cnt % 2
                cnt += 1
                pg, pgb = fetch(win[l, 56 + i * 16 + j], 2048)
                pb, pbb = fetch(wbr[l, i * 16 + j], 1024)
                for tt in range(NT):
                    b = next_bank()
                    P.mm([(ps[b][:, 0:TT], [(pg[:, k * 128:(k + 1) * 128], Hb[:, k, tsl(tt)]) for k in range(KD)])],
                         bankB[b], reads=[pgb] + [hB[k][tt] for k in range(KD)])
                    P.op("act", (lambda b=b, r=r, tt=tt, i=i, j=j: nc.scalar.activation(
                        out=SGM[r][:, tsl(tt)], in_=ps[b][:, 0:TT], func=AF.Sigmoid, bias=scol(l, O_BG + i * 16 + j), scale=1.0)),
                        reads=(bankB[b], constB), writes=(sgmB[r][tt],))
                Y, yB_ = ysrc[i]
                for tt in range(NT):
                    b = next_bank()
                    P.mm([(ps[b][:, 0:TT], [(pb[:, k * 128:(k + 1) * 128], Y[:, k, tsl(tt)]) for k in range(8)])],
                         bankB[b], reads=[pbb] + [yB_[k][tt] for k in range(8)])
                    if i == 0:
                        P.op("dve", (lambda b=b, r=r, tt=tt: nc.vector.tensor_tensor(
                            out=MF[:, tsl(tt)], in0=ps[b][:, 0:TT], in1=SGM[r][:, tsl(tt)], op=ALU.mult)),
                            reads=(bankB[b], sgmB[r][tt]), writes=(mfB[tt],))
                    else:
                        P.op("dve", (lambda b=b, r=r, tt=tt: nc.vector.tensor_tensor(
                            out=MT[:, tsl(tt)], in0=ps[b][:, 0:TT], in1=SGM[r][:, tsl(tt)], op=ALU.mult)),
                            reads=(bankB[b], sgmB[r][tt]), writes=(mtB[tt],))
                        if i == 1:
                            P.op("dve", (lambda tt=tt: nc.vector.tensor_tensor(
                                out=MF[:, tsl(tt)], in0=MF[:, tsl(tt)], in1=MT[:, tsl(tt)], op=ALU.add)),
                                reads=(mtB[tt],), writes=(mfB[tt],))
                        else:
                            P.op("dve", (lambda tt=tt, j=j: nc.vector.tensor_tensor(
                                out=MG[:, j, tsl(tt)], in0=MF[:, tsl(tt)], in1=MT[:, tsl(tt)], op=ALU.add)),
                                reads=(mtB[tt], mfB[tt]), writes=(mgB[j][tt],))

        xtemps = ([b for r_ in yaB for b in r_] + [b for r_ in ybB for b in r_] + [b for r_ in ycB for b in r_]
                  + [b for r_ in sgmB for b in r_] + mfB + mtB + [bcB, tmpsB])
        handoff(xtemps, allx)
        for k in range(KD):
            P.dma("sp", (lambda k=k: nc.sync.dma_start(out=Xf[:, k, :], in_=xsp[k])), xk_sem[k],
                  reads=(xspB[k],), writes=tuple(xB[k]))
        for j in range(KD):
            po, pob = fetch(wout[l, j], 2048)
            for tt in range(NT):
                b = next_bank()
                P.mm([(ps[b][:, 0:TT], [(po[:, k * 128:(k + 1) * 128], MG[:, k, tsl(tt)]) for k in range(KD)])],
                     bankB[b], reads=[pob] + [mgB[k][tt] for k in range(KD)])
                P.op("dve", (lambda b=b, j=j, tt=tt: nc.vector.tensor_tensor(
                    out=Xf[:, j, tsl(tt)], in0=ps[b][:, 0:TT], in1=Xf[:, j, tsl(tt)], op=ALU.add)),
                    reads=(bankB[b],), writes=(xB[j][tt],))
        return [b for r_ in mgB for b in r_]

    EPSC = MASK[:, 1:2]
    P.op("dve", lambda: nc.vector.memset(EPSC, EPS), writes=(epsB,))

    m_old = []
    for p in range(n_passes):
        for k in range(KD):
            P.dma("sp", (lambda k=k, p=p: nc.sync.dma_start(out=Xf[:, k, :], in_=xin[p, :, k * T:(k + 1) * T])),
                  xk_sem[k], writes=tuple(xB[k]))
        P.dma("sp", (lambda p=p: nc.sync.dma_start(out=MASK[:, 0:1], in_=maskd[p])), mask_sem, writes=(maskB,))
        for l in range(n_layers):
            m_old = ffn(l, 0, m_old)
            m_old = mixer(l, m_old)
            m_old = ffn(l, 1, m_old)
        m_old = norm_stats(m_old)
        outB = [Buf() for _ in range(2)]
        handoff([b for r_ in hB for b in r_], outB)
        for k in range(KD):
            r = k % 2
            P.op("dve", (lambda k=k, r=r: nc.vector.scalar_tensor_tensor(
                out=OUTR[r], in0=Xf[:, k, HALO:T], scalar=FIN[:, k:k + 1], in1=RSTD[:, HALO:T], op0=ALU.mult, op1=ALU.mult)),
                reads=(xB[k][0], xB[k][1], xB[k][2], rstdB[0], rstdB[1], rstdB[2], constB), writes=(outB[r],))
            P.dma("sp", (lambda k=k, r=r, p=p: nc.sync.dma_start(out=outd[p, k], in_=OUTR[r])), out_sem[r],
                  reads=(outB[r],))
        handoff(outB, [b for r_ in hB for b in r_])
    for r in range(2):
        P.wait("sp", (out_sem[r], P.dcnt.get(id(out_sem[r]), 0)))

    with nc.Block() as block:
        @block.tensor
        def _(e):
            P.replay("pe", e)

        @block.scalar
        def _(e):
            P.replay("act", e)

        @block.vector
        def _(e):
            P.replay("dve", e)

        @block.gpsimd
        def _(e):
            P.replay("pool", e)

        @block.sync
        def _(e):
            P.replay("sp", e)
    es.close()
    return nc, P


def _pieces(w, kin):
    Lw, K, F = w.shape
    kc = K // 128
    a = w.reshape(Lw, kc, 128, F // 128, 128)
    a = np.transpose(a, (0, 3, 2, 1, 4))
    return np.ascontiguousarray(a).reshape(Lw, F // 128, 128, kc * 128)


def prep_shared(inp):
    f = lambda a: np.asarray(a, dtype=np.float32)
    sh = {}
    sh["f1w1"] = _pieces(f(inp["ffn1_w1"]), D)
    sh["f1w3"] = _pieces(f(inp["ffn1_w3"]), D)
    sh["f1w2"] = np.ascontiguousarray(f(inp["ffn1_w2"]))
    sh["f2w1"] = _pieces(f(inp["ffn2_w1"]), D)
    sh["f2w3"] = _pieces(f(inp["ffn2_w3"]), D)
    sh["f2w2"] = np.ascontiguousarray(f(inp["ffn2_w2"]))
    sh["win"] = _pieces(f(inp["w_in"]), D)
    wb = f(inp["w_branch"])
    sh["wbr"] = _pieces(wb.reshape(L * 3, 1024, D), 1024).reshape(L, 48, 128, 1024)
    sh["wout"] = _pieces(f(inp["w_out"]), D)

    def cols(v, n):
        return np.ascontiguousarray(v.reshape(n, 128).T)

    small = np.zeros((128, L * NS), np.float32)
    bc = np.zeros((L, 128, 4096), np.float32)
    for l in range(L):
        s = small[:, l * NS:(l + 1) * NS]
        s[:, O_F1N:O_F1N + 16] = cols(f(inp["ffn1_norm"])[l], 16)
        s[:, O_MN:O_MN + 16] = cols(f(inp["mix_norm"])[l], 16)
        s[:, O_F2N:O_F2N + 16] = cols(f(inp["ffn2_norm"])[l], 16)
        bg = f(inp["b_gate"])[l]
        for i in range(3):
            s[:, O_BG + i * 16:O_BG + (i + 1) * 16] = cols(bg[i], 16)
        scw = f(inp["sconv_w"])[l]
        s[:, O_SCW:O_SCW + 24] = np.transpose(scw.reshape(3, 8, 128), (2, 1, 0)).reshape(128, 24)
        ccw = f(inp["conf_conv_w"])[l]
        s[:, O_CCW:O_CCW + 248] = np.transpose(ccw.reshape(31, 8, 128), (2, 1, 0)).reshape(128, 248)
        s[:, O_CCB:O_CCB + 8] = cols(f(inp["conf_conv_b"])[l], 8)
        s[:, O_CLG:O_CLG + 8] = cols(f(inp["conf_ln_g"])[l], 8)
        s[:, O_CLB:O_CLB + 8] = cols(f(inp["conf_ln_b"])[l], 8)
        bc[l, :, 0:1024] = f(inp["gmlp_ln_g"])[l][None, :]
        bc[l, :, 1024:2048] = f(inp["gmlp_ln_b"])[l][None, :]
        bc[l, :, 2048:3072] = f(inp["gmlp_b_s"])[l].reshape(1, 1024)
        ws = f(inp["gmlp_w_s"])[l]
        bc[l, :, 3072:4096] = np.transpose(ws, (2, 0, 1)).reshape(128, 1024)
    sh["small"] = small
    sh["bc"] = bc
    sh["fin"] = cols(f(inp["final_norm"]), 16)
    sh["ident"] = np.eye(128, dtype=np.float32)
    return sh


def prep_x(x, vs_list):
    x = np.asarray(x, dtype=np.float32)
    xin = np.zeros((len(vs_list), 128, KD * T), np.float32)
    mask = np.zeros((len(vs_list), 128, 1), np.float32)
    for i, vs in enumerate(vs_list):
        b, s0 = vs // 8, (vs % 8) * TOUT
        buf = np.zeros((T, D), np.float32)
        if s0 == 0:
            buf[HALO:] = x[b, 0:TOUT]
        else:
            buf[:] = x[b, s0 - HALO:s0 + TOUT]
            mask[i] = 1.0
        xin[i] = np.transpose(buf.reshape(T, KD, 128), (2, 1, 0)).reshape(128, KD * T)
    return xin, mask


_CACHE = {}


def kernel(**inputs):
    x = np.asarray(inputs["x"], dtype=np.float32)
    sh = prep_shared(inputs)
    if "nc" not in _CACHE:
        _CACHE["nc"] = build_program(L, 2)[0]
    nc = _CACHE["nc"]
    in_maps = []
    for c in range(NCORES):
        xin, mask = prep_x(x, [2 * c, 2 * c + 1])
        m = dict(sh)
        m["xin"] = xin
        m["mask"] = mask
        in_maps.append(m)
    res = run_bass_kernel_spmd(nc, in_maps, core_ids=list(range(NCORES)))
    out = np.empty((2, 8192, D), np.float32)
    for c in range(NCORES):
        o = np.asarray(res.results[c]["out"])
        for p in range(2):
            vs = 2 * c + p
            b, s0 = vs // 8, (vs % 8) * TOUT
            out[b, s0:s0 + TOUT, :] = np.transpose(o[p], (2, 0, 1)).reshape(TOUT, D)
    return out
```

```python
import numpy as np
import concourse.bass as bass
import concourse.mybir as mybir
from concourse.bass_utils import run_bass_kernel_spmd

F32 = mybir.dt.float32
BF16 = mybir.dt.bfloat16
AF = mybir.ActivationFunctionType
ALU = mybir.AluOpType

D = 2048
KD = 16
DFF = 5632
NFC = 44
NGRP = 11
TOUT = 1024
GEO = [(1280, 5, 256, 256), (1024, 4, 256, 0)]
TMAX = 1280
L = 4
NSLOT = 6
NCORES = 8
EPS = 1e-6

O_F1N, O_MN, O_F2N, O_BG, O_SCW, O_CCW, O_CCB, O_CLG, O_CLB = 0, 16, 32, 48, 96, 120, 368, 376, 384
NS = 392


class Buf:
    __slots__ = ("w", "r")

    def __init__(self):
        self.w = None
        self.r = {}

    def toks(self):
        out = list(self.r.values())
        if self.w is not None:
            out.append(self.w)
        return out


class Prog:
    ENG = ("pe", "act", "dve", "pool", "sp")

    def __init__(self, nc, sems):
        self.nc = nc
        self.q = {e: [] for e in self.ENG}
        self.sem = sems
        self.cnt = {e: 0 for e in self.ENG}
        self.seen = {e: {} for e in self.ENG}
        self.dcnt = {}

    def wait(self, eng, tok):
        if tok is None:
            return
        sem, val = tok
        key = id(sem)
        if self.seen[eng].get(key, 0) >= val:
            return
        self.seen[eng][key] = val
        self.q[eng].append(("w", sem, val))

    def _deps(self, eng, reads, writes, extra):
        for b in reads:
            self.wait(eng, b.w)
        for b in writes:
            self.wait(eng, b.w)
            for t in b.r.values():
                self.wait(eng, t)
        for t in extra:
            self.wait(eng, t)

    def _note(self, tok, reads, writes):
        key = id(tok[0])
        for b in reads:
            old = b.r.get(key)
            if old is None or old[1] < tok[1]:
                b.r[key] = tok
        for b in writes:
            b.w = tok
            b.r = {}

    def op(self, eng, fn, reads=(), writes=(), extra=()):
        self._deps(eng, reads, writes, extra)
        self.cnt[eng] += 1
        tok = (self.sem[eng], self.cnt[eng])
        self.q[eng].append(("i", fn, 1))
        self._note(tok, reads, writes)
        return tok

    def mm(self, groups, bank, reads):
        self._deps("pe", reads, (bank,), ())
        flat = []
        for out_ap, mms in groups:
            n = len(mms)
            for i, (lt, rh) in enumerate(mms):
                flat.append((out_ap, lt, rh, i == 0, i == n - 1))
        self.cnt["pe"] += 1
        tok = (self.sem["pe"], self.cnt["pe"])
        nc = self.nc
        for idx, (o, lt, rh, st, sp) in enumerate(flat):
            last = idx == len(flat) - 1
            self.q["pe"].append(("i", (lambda o=o, lt=lt, rh=rh, st=st, sp=sp:
                                       nc.tensor.matmul(o, lhsT=lt, rhs=rh, start=st, stop=sp)), 1 if last else 0))
        self._note(tok, reads, (bank,))
        return tok

    def dma(self, eng, fn, sem, reads=(), writes=(), extra=()):
        self._deps(eng, reads, writes, extra)
        self.dcnt[id(sem)] = self.dcnt.get(id(sem), 0) + 16
        tok = (sem, self.dcnt[id(sem)])
        self.q[eng].append(("d", fn, sem))
        self._note(tok, reads, writes)
        return tok

    def replay(self, eng, engobj):
        sem = self.sem[eng]
        for item in self.q[eng]:
            if item[0] == "w":
                engobj.wait_ge(item[1], item[2])
            elif item[0] == "i":
                ins = item[1]()
                if item[2]:
                    ins.then_inc(sem, 1)
            else:
                item[1]().then_inc(item[2], 16)


def handoff(old_bufs, new_bufs):
    merged = {}
    for b in old_bufs:
        for t in b.toks():
            k = id(t[0])
            if k not in merged or merged[k][1] < t[1]:
                merged[k] = t
    for nb in new_bufs:
        for k, t in merged.items():
            o = nb.r.get(k)
            if o is None or o[1] < t[1]:
                nb.r[k] = t


def build_program(n_layers=L, n_passes=2):
    nc = bass.Bass("TRN2", target_bir_lowering=False)

    def din(name, shape):
        return nc.dram_tensor(name, shape, F32, kind="ExternalInput").ap()

    xin = [din(f"xin{p}", [128, KD * GEO[p][0]]) for p in range(n_passes)]
    maskd = din("mask", [128, 1])
    fw1 = [din("f1w1", [L, NFC, 128, 2048]), din("f2w1", [L, NFC, 128, 2048])]
    fw3 = [din("f1w3", [L, NFC, 128, 2048]), din("f2w3", [L, NFC, 128, 2048])]
    fw2 = [din("f1w2", [L, DFF, D]), din("f2w2", [L, DFF, D])]
    win = din("win", [L, 104, 128, 2048])
    wbr = din("wbr", [L, 48, 128, 1024])
    wout = din("wout", [L, 16, 128, 2048])
    smalld = din("small", [128, L * NS])
    bcd = din("bc", [L, 128, 4096])
    find = din("fin", [128, 16])
    identd = din("ident", [128, 128])
    xsp = nc.dram_tensor("xspill", [KD, 128, TMAX], F32, kind="Internal").ap()
    outd = nc.dram_tensor("out", [n_passes, KD, 128, TOUT], F32, kind="ExternalOutput").ap()

    W_X, W_H, W_M = KD * TMAX, KD * TMAX // 2, KD * TMAX // 2
    sizes = dict(X=W_X, H=W_H, M=W_M, RING=NSLOT * 1024, SMALL=L * NS, FIN=16, WST=512, RSTD=TMAX, ONES=64, IDENT=64,
                 MASK=2, SQT=512, SA=512, STA=10 * 12, MVA=20, RSA=20, TAILZ=L * 8 * 15, TAILS=L * 8 * 2)
    NBIG = sum(sizes.values())

    import contextlib
    es = contextlib.ExitStack()
    big = es.enter_context(nc.sbuf_tensor("big", [128, NBIG], F32))
    ps = [es.enter_context(nc.psum_tensor(f"ps{i}", [128, 512], F32)) for i in range(8)]
    esem = {e: es.enter_context(nc.semaphore(f"s_{e}")) for e in Prog.ENG}
    ring_sem = [es.enter_context(nc.semaphore(f"ring{i}")) for i in range(NSLOT)]
    xk_sem = [es.enter_context(nc.semaphore(f"xk{i}")) for i in range(KD)]
    out_sem = [es.enter_context(nc.semaphore(f"o{i}")) for i in range(2)]
    setup_sem = es.enter_context(nc.semaphore("setup"))
    bc_sem = es.enter_context(nc.semaphore("bcs"))
    mask_sem = es.enter_context(nc.semaphore("masks"))

    off = {}
    o = 0
    for k_, v_ in sizes.items():
        off[k_] = o
        o += v_

    def reg(name, a=0, n=None):
        n = sizes[name] - a if n is None else n
        return big[:, off[name] + a: off[name] + a + n]

    ringv = [reg("RING", s * 1024, 1024).bitcast(BF16) for s in range(NSLOT)]
    SMALL = reg("SMALL")
    FIN = reg("FIN")
    WST = reg("WST").bitcast(BF16)
    RSTD = reg("RSTD")
    ONES = reg("ONES").bitcast(BF16)
    IDENT = reg("IDENT").bitcast(BF16)
    MASK = reg("MASK")
    EPSC = MASK[:, 1:2]
    SQT = reg("SQT")
    SA = reg("SA")
    RSA = reg("RSA")
    TAILZ = reg("TAILZ").bitcast(BF16).rearrange("p (l j t) -> p l j t", l=L, j=8)
    TAILS = reg("TAILS").rearrange("p (l j t) -> p l j t", l=L, j=8)

    P = Prog(nc, esem)

    bankB = [Buf() for _ in range(8)]
    slotB = [Buf() for _ in range(NSLOT)]
    constB, epsB, maskB, wstB, statB, onesB = Buf(), Buf(), Buf(), Buf(), Buf(), Buf()
    sqtB, saB = Buf(), Buf()
    xspB = [Buf() for _ in range(KD)]
    tailzB = [[Buf() for _ in range(8)] for _ in range(L)]
    tailsB = [[Buf() for _ in range(8)] for _ in range(L)]
    st = dict(bank=0, piece=0)

    def next_bank():
        b = st["bank"] % 8
        st["bank"] += 1
        return b

    def fetch(dram2d, n):
        s = st["piece"] % NSLOT
        st["piece"] += 1
        dst = ringv[s][:, 0:n]
        P.dma("pool", (lambda dst=dst, src=dram2d: nc.gpsimd.dma_start(out=dst, in_=src)),
              ring_sem[s], writes=(slotB[s],))
        return ringv[s], slotB[s]

    def scol(l, o_, n=1):
        return SMALL[:, l * NS + o_: l * NS + o_ + n]

    P.dma("sp", lambda: nc.sync.dma_start(out=SMALL, in_=smalld), setup_sem, writes=(constB,))
    P.dma("sp", lambda: nc.sync.dma_start(out=FIN, in_=find), setup_sem, writes=(constB,))
    P.dma("pool", lambda: nc.gpsimd.dma_start(out=IDENT, in_=identd), setup_sem, writes=(constB,))
    constB.w = (setup_sem, 48)
    P.op("dve", lambda: nc.vector.memset(ONES, 1.0), writes=(onesB,))
    P.op("dve", lambda: nc.vector.memset(EPSC, EPS), writes=(epsB,))

    def run_pass(p, barrier):
        T, NT, TT, HALO = GEO[p]
        NBLK = T // 128
        first = (p == 0)

        def newbuf():
            b = Buf()
            for t in barrier:
                b.r[id(t[0])] = t
            return b

        def tsl(tt):
            return slice(tt * TT, (tt + 1) * TT)

        def tiles_of(a, b):
            return list(range(a // TT, (b - 1) // TT + 1))

        Xf = big[:, off["X"]: off["X"] + KD * T].rearrange("p (k t) -> p k t", k=KD)
        Hb = big[:, off["H"]: off["H"] + KD * T // 2].bitcast(BF16).rearrange("p (k t) -> p k t", k=KD)
        Mreg = big[:, off["M"]: off["M"] + KD * T // 2]
        SLOTW = 4 * T

        def xslot(i, a=0, n=SLOTW):
            return big[:, off["X"] + i * SLOTW + a: off["X"] + i * SLOTW + a + n]

        YA = xslot(0).bitcast(BF16).rearrange("p (k t) -> p k t", k=8)
        YB = xslot(1).bitcast(BF16).rearrange("p (k t) -> p k t", k=8)
        YC = xslot(2).bitcast(BF16).rearrange("p (k t) -> p k t", k=8)
        BCT = xslot(2, 0, 4096)
        TMPS = xslot(1, 0, 512)
        VLN = xslot(3).bitcast(BF16).rearrange("p (n f) -> p n f", n=NBLK)
        ZBW = (T + 32) // 2
        SGC = xslot(3, 0, T)
        ZB = [xslot(3, T + i * ZBW, ZBW).bitcast(BF16) for i in range(2)]
        DIAG = xslot(3, T + 2 * ZBW, 1984).bitcast(BF16)
        SS = xslot(3, 0, T + 2)
        CS = xslot(3, T + 2, T)
        CG = CS
        WC = TT
        assert 2 * T + 2 <= SLOTW - 1024 and WC == 256
        ZCB = big[:, off["SQT"]: off["SQT"] + 1024].bitcast(BF16).rearrange("p (j t) -> p j t", j=8)
        ZSQ = xslot(3, SLOTW - 1024, 1024).bitcast(BF16).rearrange("p (j t) -> p j t", j=8)
        LT = [big[:, off["RSTD"] + i * WC: off["RSTD"] + (i + 1) * WC] for i in range(5)]
        SGM = [xslot(3, i * (T // 2), T // 2).bitcast(BF16) for i in range(2)]
        MF = xslot(3, T, T)
        MT = xslot(3, 2 * T, T)
        GR = Mreg[:, 0:4 * T].bitcast(BF16).rearrange("p (s c t) -> p s c t", s=2, c=4)
        XSQ2 = [big[:, off["M"] + 4 * T + r_ * 2048: off["M"] + 4 * T + (r_ + 1) * 2048].bitcast(BF16)
                .rearrange("p (k t) -> p k t", k=KD) for r_ in range(2)]
        assert TT == 256 and 4 * T + 4096 <= KD * T // 2
        VG = Mreg.rearrange("p (n f) -> p n f", n=NBLK)
        ZC = Mreg.rearrange("p (j t) -> p j t", j=8)
        MG = Mreg.bitcast(BF16).rearrange("p (k t) -> p k t", k=KD)
        OUTR = [big[:, off["H"] + i * TOUT: off["H"] + (i + 1) * TOUT] for i in range(2)]
        STA = reg("STA", 0, NBLK * 12).rearrange("p (n s) -> p n s", n=NBLK)
        MVA = reg("MVA", 0, NBLK * 2).rearrange("p (n s) -> p n s", n=NBLK)

        xB = [[newbuf() for _ in range(NT)] for _ in range(KD)]
        hB = [[newbuf() for _ in range(NT)] for _ in range(KD)]
        rstdB = [newbuf() for _ in range(NT)]
        xsqB2 = [[newbuf() for _ in range(KD)] for _ in range(2)]
        xsqB = [b for r_ in xsqB2 for b in r_]
        nst = dict(n=0)
        allx = [b for r_ in xB for b in r_]
        allh = [b for r_ in hB for b in r_]

        def proj_ft(piece, pbuf, tt, nk=KD):
            b = next_bank()
            P.mm([(ps[b][:, 0:TT], [(piece[:, k * 128:(k + 1) * 128], Hb[:, k, tsl(tt)]) for k in range(nk)])],
                 bankB[b], reads=[pbuf] + [hB[k][tt] for k in range(nk)])
            return b

        def norm_tile(tt, l, ocol, make_h=True):
            r = nst["n"] % 2
            nst["n"] += 1
            XS = XSQ2[r]
            for k in range(KD):
                P.op("act", (lambda k=k, tt=tt, XS=XS: nc.scalar.activation(out=XS[:, k, :], in_=Xf[:, k, tsl(tt)],
                                                                           func=AF.Square)),
                     reads=(xB[k][tt],), writes=(xsqB2[r][k],))
            b = next_bank()
            P.mm([(ps[b][:, 0:TT], [(ONES, XS[:, k, :]) for k in range(KD)])], bankB[b], reads=xsqB2[r] + [onesB])
            P.op("act", (lambda b=b: nc.scalar.activation(out=SQT[:, 0:TT], in_=ps[b][:, 0:TT], func=AF.Sqrt,
                                                          bias=EPSC, scale=1.0 / D)),
                 reads=(bankB[b], epsB), writes=(sqtB,))
            P.op("dve", (lambda tt=tt: nc.vector.reciprocal(out=RSTD[:, tsl(tt)], in_=SQT[:, 0:TT])),
                 reads=(sqtB,), writes=(rstdB[tt],))
            if make_h:
                for k in range(KD):
                    P.op("dve", (lambda k=k, tt=tt: nc.vector.scalar_tensor_tensor(
                        out=Hb[:, k, tsl(tt)], in0=Xf[:, k, tsl(tt)], scalar=scol(l, ocol + k),
                        in1=RSTD[:, tsl(tt)], op0=ALU.mult, op1=ALU.mult)),
                        reads=(xB[k][tt], rstdB[tt], constB), writes=(hB[k][tt],))

        def ffn(l, which, m_old_bufs, do_norm, nxt):
            ocol = O_F1N if which == 0 else O_F2N
            if do_norm:
                handoff(m_old_bufs, xsqB)
                for tt in range(NT):
                    norm_tile(tt, l, ocol)
            gB = [[[newbuf() for _ in range(NT)] for _ in range(4)] for _ in range(2)]
            gall = [b for s_ in gB for c_ in s_ for b in c_]
            handoff(m_old_bufs, gall)
            w1, w3, w2 = fw1[which], fw3[which], fw2[which]

            def p1(G):
                s = G % 2
                for ci in range(4):
                    c = G * 4 + ci
                    pa, pab = fetch(w1[l, c], 2048)
                    pb, pbb = fetch(w3[l, c], 2048)
                    for tt in range(NT):
                        ba = proj_ft(pa, pab, tt)
                        bb = proj_ft(pb, pbb, tt)
                        P.op("act", (lambda ba=ba: nc.scalar.activation(out=SA[:, 0:TT], in_=ps[ba][:, 0:TT], func=AF.Silu)),
                             reads=(bankB[ba],), writes=(saB,))
                        P.op("dve", (lambda bb=bb, s=s, ci=ci, tt=tt: nc.vector.tensor_tensor(
                            out=GR[:, s, ci, tsl(tt)], in0=ps[bb][:, 0:TT], in1=SA[:, 0:TT], op=ALU.mult)),
                            reads=(bankB[bb], saB), writes=(gB[s][ci][tt],))

            def p2(G, last):
                s = G % 2
                pw = [fetch(w2[l, (G * 4 + ci) * 128:(G * 4 + ci + 1) * 128, :], 2048) for ci in range(4)]
                order = ([(j, tt) for tt in range(NT) for j in range(KD)] if last
                         else [(j, tt) for j in range(KD) for tt in range(NT)])
                for (j, tt) in order:
                    if True:
                        b = next_bank()
                        P.mm([(ps[b][:, 0:TT], [(pw[ci][0][:, j * 128:(j + 1) * 128], GR[:, s, ci, tsl(tt)])
                                                for ci in range(4)])],
                             bankB[b], reads=[pw[ci][1] for ci in range(4)] + [gB[s][ci][tt] for ci in range(4)])
                        P.op("dve", (lambda b=b, j=j, tt=tt: nc.vector.scalar_tensor_tensor(
                            out=Xf[:, j, tsl(tt)], in0=ps[b][:, 0:TT], scalar=0.5, in1=Xf[:, j, tsl(tt)],
                            op0=ALU.mult, op1=ALU.add)),
                            reads=(bankB[b],), writes=(xB[j][tt],))
                    if last and j == KD - 1 and nxt is not None:
                        nxt(tt)

            for G in range(NGRP + 1):
                if G < NGRP:
                    p1(G)
                if G >= 1:
                    p2(G - 1, G - 1 == NGRP - 1)
            return gall + xsqB

        def mixer(l, m_old_bufs):
            xs = m_old_bufs
            for k in range(KD):
                P.dma("sp", (lambda k=k: nc.sync.dma_start(out=xsp[k][:, 0:T], in_=Xf[:, k, :])), xk_sem[k],
                      reads=tuple(xB[k]), writes=(xspB[k],))
            yaB = [[newbuf() for _ in range(NT)] for _ in range(8)]
            ybB = [[newbuf() for _ in range(NT)] for _ in range(8)]
            ycB = [[newbuf() for _ in range(NT)] for _ in range(8)]
            bcB, tmpsB = newbuf(), newbuf()
            vlnB = [newbuf() for _ in range(NBLK)]
            handoff(allx, [b for r_ in yaB for b in r_] + [bcB] + vlnB + [tmpsB])
            P.dma("sp", (lambda: nc.sync.dma_start(out=BCT, in_=bcd[l])), bc_sem, writes=(bcB,))
            P.op("dve", lambda: nc.vector.tensor_copy(out=WST, in_=BCT[:, 3072:4096]), reads=(bcB,), writes=(wstB,))
            wst3 = WST.rearrange("p (g i) -> p g i", g=8)
            P.op("dve", lambda: nc.vector.memset(wst3[64:128, :, 0:64], 0.0), writes=(wstB,))

            vgrp = [list(range(a, min(a + 4, NBLK))) for a in range(0, NBLK, 4)]
            vgB = [[newbuf() for _ in range(8)] for _ in vgrp]
            handoff(xs, [b for r_ in vgB for b in r_])
            for c in range(8):
                pv, pvb = fetch(win[l, 8 + c], 2048)
                for gi, blks in enumerate(vgrp):
                    b = next_bank()
                    groups = []
                    for bi, n in enumerate(blks):
                        groups.append((ps[b][:, bi * 128:(bi + 1) * 128],
                                       [(Hb[:, k, n * 128:(n + 1) * 128], pv[:, k * 128:(k + 1) * 128]) for k in range(KD)]))
                    tts = tiles_of(blks[0] * 128, (blks[-1] + 1) * 128)
                    P.mm(groups, bankB[b], reads=[pvb] + [hB[k][t_] for k in range(KD) for t_ in tts])
                    nb = len(blks)
                    P.op("act", (lambda b=b, n0=blks[0], nb=nb, c=c: nc.scalar.activation(
                        out=VG[:, n0:n0 + nb, c * 128:(c + 1) * 128],
                        in_=ps[b][:, 0:nb * 128].rearrange("p (n f) -> p n f", n=nb), func=AF.Gelu)),
                        reads=(bankB[b],), writes=(vgB[gi][c],))
            for n in range(NBLK):
                rd = tuple(vgB[n // 4])
                P.op("dve", (lambda n=n: nc.vector.bn_stats(out=STA[:, n, 0:6], in_=VG[:, n, 0:512])), reads=rd, writes=(statB,))
                P.op("dve", (lambda n=n: nc.vector.bn_stats(out=STA[:, n, 6:12], in_=VG[:, n, 512:1024])), reads=rd, writes=(statB,))
                P.op("dve", (lambda n=n: nc.vector.bn_aggr(out=MVA[:, n, :], in_=STA[:, n, :])), reads=(statB,), writes=(statB,))
            P.op("act", (lambda: nc.scalar.activation(out=RSA[:, 0:NBLK], in_=MVA[:, :, 1], func=AF.Sqrt, bias=EPSC, scale=1.0)),
                 reads=(statB, epsB), writes=(statB,))
            P.op("dve", (lambda: nc.vector.reciprocal(out=RSA[:, 10:10 + NBLK], in_=RSA[:, 0:NBLK])), reads=(statB,), writes=(statB,))
            for c in range(8):
                pu, pub = fetch(win[l, c], 2048)
                for tt in range(NT):
                    b = proj_ft(pu, pub, tt)
                    P.op("act", (lambda b=b, c=c, tt=tt: nc.scalar.activation(out=YA[:, c, tsl(tt)], in_=ps[b][:, 0:TT], func=AF.Gelu)),
                         reads=(bankB[b],), writes=(yaB[c][tt],))
            for n in range(NBLK):
                rd = tuple(vgB[n // 4])
                P.op("dve", (lambda n=n: nc.vector.tensor_scalar(out=VG[:, n, :], in0=VG[:, n, :], scalar1=MVA[:, n, 0:1],
                                                                 scalar2=RSA[:, 10 + n:10 + n + 1],
                                                                 op0=ALU.subtract, op1=ALU.mult)),
                     reads=(statB,), writes=rd)
                P.op("dve", (lambda n=n: nc.vector.tensor_tensor(out=VG[:, n, :], in0=VG[:, n, :], in1=BCT[:, 0:1024], op=ALU.mult)),
                     reads=(bcB,), writes=rd)
                P.op("dve", (lambda n=n: nc.vector.tensor_tensor(out=VLN[:, n, :], in0=VG[:, n, :], in1=BCT[:, 1024:2048], op=ALU.add)),
                     reads=rd + (bcB,), writes=(vlnB[n],))
            for n in range(NBLK):
                tts = tiles_of(n * 128, (n + 1) * 128)
                for half in range(2):
                    b = next_bank()
                    groups = []
                    for gi in range(4):
                        g = half * 4 + gi
                        groups.append((ps[b][:, gi * 128:(gi + 1) * 128],
                                       [(VLN[:, n, g * 128:(g + 1) * 128], WST[:, g * 128:(g + 1) * 128])]))
                    P.mm(groups, bankB[b], reads=[vlnB[n], wstB])
                    P.op("dve", (lambda b=b, half=half: nc.vector.tensor_tensor(
                        out=TMPS, in0=ps[b][:, 0:512], in1=BCT[:, 2048 + half * 512: 2048 + (half + 1) * 512], op=ALU.add)),
                        reads=(bankB[b], bcB), writes=(tmpsB,))
                    P.op("dve", (lambda half=half, n=n: nc.vector.tensor_tensor(
                        out=YA[:, half * 4:(half + 1) * 4, n * 128:(n + 1) * 128],
                        in0=TMPS.rearrange("p (g i) -> p g i", g=4),
                        in1=YA[:, half * 4:(half + 1) * 4, n * 128:(n + 1) * 128], op=ALU.mult)),
                        reads=(tmpsB,), writes=tuple(yaB[half * 4 + gi][t_] for gi in range(4) for t_ in tts))

            zcB = [[newbuf() for _ in range(NT)] for _ in range(8)]
            handoff([b for r_ in vgB for b in r_], [b for r_ in zcB for b in r_])
            sgB = [newbuf() for _ in range(NT)]
            zbB = [newbuf() for _ in range(2)]
            diagB = newbuf()
            handoff(vlnB, sgB + zbB + [diagB])
            pend = None

            def conv_c(j, r):
                for tt in range(NT):
                    b = next_bank()
                    P.mm([(ps[b][:, 0:TT], [(DIAG[:, kk * 128:(kk + 1) * 128], ZB[r][:, kk + tt * TT: kk + tt * TT + TT])
                                            for kk in range(31)])], bankB[b], reads=[diagB, zbB[r]])
                    P.op("act", (lambda b=b, j=j, tt=tt: nc.scalar.activation(
                        out=ZC[:, j, tsl(tt)], in_=ps[b][:, 0:TT], func=AF.Identity, bias=scol(l, O_CCB + j), scale=1.0)),
                        reads=(bankB[b], constB), writes=(zcB[j][tt],))

            for j in range(8):
                r = j % 2
                pg, pgb = fetch(win[l, 48 + j], 2048)
                pvv, pvvb = fetch(win[l, 40 + j], 2048)
                for tt in range(NT):
                    b = proj_ft(pg, pgb, tt)
                    P.op("act", (lambda b=b, tt=tt: nc.scalar.activation(out=SGC[:, tsl(tt)], in_=ps[b][:, 0:TT], func=AF.Sigmoid)),
                         reads=(bankB[b],), writes=(sgB[tt],))
                if first:
                    P.op("dve", (lambda r=r: nc.vector.memset(ZB[r][:, 0:30], 0.0)), writes=(zbB[r],))
                else:
                    P.op("dve", (lambda r=r, j=j: nc.vector.tensor_copy(out=ZB[r][:, 0:30], in_=TAILZ[:, l, j, :])),
                         reads=(tailzB[l][j],), writes=(zbB[r],))
                for tt in range(NT):
                    b = proj_ft(pvv, pvvb, tt)
                    P.op("dve", (lambda b=b, tt=tt, r=r: nc.vector.tensor_tensor(
                        out=ZB[r][:, 30 + tt * TT: 30 + (tt + 1) * TT], in0=ps[b][:, 0:TT], in1=SGC[:, tsl(tt)], op=ALU.mult)),
                        reads=(bankB[b], sgB[tt]), writes=(zbB[r],))
                if HALO:
                    P.op("dve", (lambda r=r: nc.vector.tensor_scalar(out=ZB[r][:, 30:30 + HALO], in0=ZB[r][:, 30:30 + HALO],
                                                                     scalar1=MASK[:, 0:1], scalar2=None, op0=ALU.mult)),
                         reads=(maskB,), writes=(zbB[r],))
                if first and n_passes > 1:
                    P.op("dve", (lambda r=r, j=j: nc.vector.tensor_copy(out=TAILZ[:, l, j, :], in_=ZB[r][:, T:T + 30])),
                         reads=(zbB[r],), writes=(tailzB[l][j],))
                if pend is not None:
                    conv_c(*pend)
                for kk in range(31):
                    P.op("act", (lambda kk=kk, j=j: nc.scalar.activation(out=DIAG[:, kk * 128:(kk + 1) * 128], in_=IDENT, func=AF.Copy,
                                                                        scale=scol(l, O_CCW + j * 31 + kk))),
                         reads=(constB,), writes=(diagB,))
                pend = (j, r)
            conv_c(*pend)

            cgB = [newbuf() for _ in range(NT)]
            ssB, csB = newbuf(), newbuf()
            zcbB = [newbuf() for _ in range(8)]
            zsqB = [newbuf() for _ in range(8)]
            ltB = [newbuf() for _ in range(5)]
            handoff(sgB + zbB + [diagB], cgB + [ssB, csB] + zsqB)
            handoff(rstdB + [sqtB, saB], ltB + zcbB)
            handoff([tmpsB], [b for r_ in ybB for b in r_])
            handoff([bcB], [b for r_ in ycB for b in r_])

            def lnc_pre(q):
                qs = slice(q * WC, (q + 1) * WC)
                for j in range(8):
                    P.op("act", (lambda j=j, qs=qs: nc.scalar.activation(out=ZCB[:, j, :], in_=ZC[:, j, qs], func=AF.Copy)),
                         reads=(zcB[j][q],), writes=(zcbB[j],))
                    P.op("act", (lambda j=j, qs=qs: nc.scalar.activation(out=ZSQ[:, j, :], in_=ZC[:, j, qs], func=AF.Square)),
                         reads=(zcB[j][q],), writes=(zsqB[j],))

            def lnc_post(q):
                qs = slice(q * WC, (q + 1) * WC)
                mt = q
                b1 = next_bank()
                P.mm([(ps[b1][:, 0:WC], [(ONES, ZCB[:, j, :]) for j in range(8)])], bankB[b1], reads=zcbB + [onesB])
                b2 = next_bank()
                P.mm([(ps[b2][:, 0:WC], [(ONES, ZSQ[:, j, :]) for j in range(8)])], bankB[b2], reads=zsqB + [onesB])
                P.op("act", (lambda b1=b1: nc.scalar.activation(out=LT[0], in_=ps[b1][:, 0:WC], func=AF.Square, scale=1.0 / 1024)),
                     reads=(bankB[b1],), writes=(ltB[0],))
                P.op("dve", (lambda b2=b2: nc.vector.scalar_tensor_tensor(out=LT[1], in0=ps[b2][:, 0:WC], scalar=1.0 / 1024,
                                                                          in1=LT[0], op0=ALU.mult, op1=ALU.subtract)),
                     reads=(bankB[b2], ltB[0]), writes=(ltB[1],))
                P.op("act", (lambda: nc.scalar.activation(out=LT[0], in_=LT[1], func=AF.Sqrt, bias=EPSC, scale=1.0)),
                     reads=(ltB[1], epsB), writes=(ltB[0],))
                P.op("dve", (lambda: nc.vector.reciprocal(out=LT[4], in_=LT[0])), reads=(ltB[0],), writes=(ltB[4],))
                for j in range(8):
                    P.op("dve", (lambda b1=b1, j=j, qs=qs: nc.vector.scalar_tensor_tensor(
                        out=LT[2], in0=ps[b1][:, 0:WC], scalar=-1.0 / 1024, in1=ZC[:, j, qs], op0=ALU.mult, op1=ALU.add)),
                        reads=(bankB[b1], zcB[j][mt]), writes=(ltB[2],))
                    P.op("dve", (lambda: nc.vector.tensor_tensor(out=LT[3], in0=LT[2], in1=LT[4], op=ALU.mult)),
                         reads=(ltB[2], ltB[4]), writes=(ltB[3],))
                    P.op("act", (lambda j=j, qs=qs: nc.scalar.activation(out=YC[:, j, qs], in_=LT[3], func=AF.Silu,
                                                                        bias=scol(l, O_CLB + j), scale=scol(l, O_CLG + j))),
                         reads=(ltB[3], constB), writes=(ycB[j][mt],))

            for j in range(8):
                if j < NT:
                    lnc_pre(j)
                pc, pcb = fetch(win[l, 24 + j], 2048)
                px, pxb = fetch(win[l, 32 + j], 2048)
                pbg, pbgb = fetch(win[l, 16 + j], 2048)
                for tt in range(NT):
                    b = proj_ft(pc, pcb, tt)
                    P.op("act", (lambda b=b, tt=tt: nc.scalar.activation(out=CG[:, tsl(tt)], in_=ps[b][:, 0:TT], func=AF.Copy)),
                         reads=(bankB[b],), writes=(cgB[tt], csB))
                if first:
                    P.op("dve", (lambda: nc.vector.memset(SS[:, 0:2], 0.0)), writes=(ssB,))
                else:
                    P.op("dve", (lambda j=j: nc.vector.tensor_copy(out=SS[:, 0:2], in_=TAILS[:, l, j, :])),
                         reads=(tailsB[l][j],), writes=(ssB,))
                for tt in range(NT):
                    b = proj_ft(px, pxb, tt)
                    P.op("dve", (lambda b=b, tt=tt: nc.vector.tensor_tensor(
                        out=SS[:, 2 + tt * TT: 2 + (tt + 1) * TT], in0=ps[b][:, 0:TT], in1=CG[:, tsl(tt)], op=ALU.mult)),
                        reads=(bankB[b], cgB[tt]), writes=(ssB,))
                if HALO:
                    P.op("dve", (lambda: nc.vector.tensor_scalar(out=SS[:, 2:2 + HALO], in0=SS[:, 2:2 + HALO], scalar1=MASK[:, 0:1],
                                                                 scalar2=None, op0=ALU.mult)), reads=(maskB,), writes=(ssB,))
                if first and n_passes > 1:
                    P.op("dve", (lambda j=j: nc.vector.tensor_copy(out=TAILS[:, l, j, :], in_=SS[:, T:T + 2])),
                         reads=(ssB,), writes=(tailsB[l][j],))
                P.op("dve", (lambda j=j: nc.vector.tensor_scalar(out=CS, in0=SS[:, 0:T], scalar1=scol(l, O_SCW + j * 3 + 0),
                                                                 scalar2=None, op0=ALU.mult)), reads=(ssB, constB), writes=(csB,))
                for kk in (1, 2):
                    P.op("dve", (lambda j=j, kk=kk: nc.vector.scalar_tensor_tensor(
                        out=CS, in0=SS[:, kk:kk + T], scalar=scol(l, O_SCW + j * 3 + kk), in1=CS, op0=ALU.mult, op1=ALU.add)),
                        reads=(ssB, constB), writes=(csB,))
                for tt in range(NT):
                    b = proj_ft(pbg, pbgb, tt)
                    P.op("dve", (lambda b=b, tt=tt, j=j: nc.vector.tensor_tensor(
                        out=YB[:, j, tsl(tt)], in0=ps[b][:, 0:TT], in1=CS[:, tsl(tt)], op=ALU.mult)),
                        reads=(bankB[b], csB), writes=(ybB[j][tt],))
                if j < NT:
                    lnc_post(j)

            mgB = [[newbuf() for _ in range(NT)] for _ in range(KD)]
            handoff([b for r_ in zcB for b in r_], [b for r_ in mgB for b in r_])
            sgmB = [[newbuf() for _ in range(NT)] for _ in range(2)]
            mfB = [newbuf() for _ in range(NT)]
            mtB = [newbuf() for _ in range(NT)]
            handoff(cgB + [ssB, csB] + zsqB, [b for r_ in sgmB for b in r_] + mfB + mtB)
            handoff(ltB + zcbB, rstdB + [sqtB, saB])
            ysrc = [(YA, yaB), (YB, ybB), (YC, ycB)]
            cnt = 0
            for j in range(KD):
                for i in range(3):
                    r = cnt % 2
                    cnt += 1
                    pg, pgb = fetch(win[l, 56 + i * 16 + j], 2048)
                    pb, pbb = fetch(wbr[l, i * 16 + j], 1024)
                    for tt in range(NT):
                        b = proj_ft(pg, pgb, tt)
                        P.op("act", (lambda b=b, r=r, tt=tt, i=i, j=j: nc.scalar.activation(
                            out=SGM[r][:, tsl(tt)], in_=ps[b][:, 0:TT], func=AF.Sigmoid, bias=scol(l, O_BG + i * 16 + j), scale=1.0)),
                            reads=(bankB[b], constB), writes=(sgmB[r][tt],))
                    Y, yB_ = ysrc[i]
                    for tt in range(NT):
                        b = next_bank()
                        P.mm([(ps[b][:, 0:TT], [(pb[:, k * 128:(k + 1) * 128], Y[:, k, tsl(tt)]) for k in range(8)])],
                             bankB[b], reads=[pbb] + [yB_[k][tt] for k in range(8)])
                        if i == 0:
                            P.op("dve", (lambda b=b, r=r, tt=tt: nc.vector.tensor_tensor(
                                out=MF[:, tsl(tt)], in0=ps[b][:, 0:TT], in1=SGM[r][:, tsl(tt)], op=ALU.mult)),
                                reads=(bankB[b], sgmB[r][tt]), writes=(mfB[tt],))
                        else:
                            P.op("dve", (lambda b=b, r=r, tt=tt: nc.vector.tensor_tensor(
                                out=MT[:, tsl(tt)], in0=ps[b][:, 0:TT], in1=SGM[r][:, tsl(tt)], op=ALU.mult)),
                                reads=(bankB[b], sgmB[r][tt]), writes=(mtB[tt],))
                            if i == 1:
                                P.op("dve", (lambda tt=tt: nc.vector.tensor_tensor(
                                    out=MF[:, tsl(tt)], in0=MF[:, tsl(tt)], in1=MT[:, tsl(tt)], op=ALU.add)),
                                    reads=(mtB[tt],), writes=(mfB[tt],))
                            else:
                                P.op("dve", (lambda tt=tt, j=j: nc.vector.tensor_tensor(
                                    out=MG[:, j, tsl(tt)], in0=MF[:, tsl(tt)], in1=MT[:, tsl(tt)], op=ALU.add)),
                                    reads=(mtB[tt], mfB[tt]), writes=(mgB[j][tt],))

            xtemps = ([b for r_ in yaB for b in r_] + [b for r_ in ybB for b in r_] + [b for r_ in ycB for b in r_]
                      + [b for r_ in sgmB for b in r_] + mfB + mtB + [bcB, tmpsB])
            handoff(xtemps, allx)
            for k in range(KD):
                P.dma("sp", (lambda k=k: nc.sync.dma_start(out=Xf[:, k, :], in_=xsp[k][:, 0:T])), xk_sem[k],
                      reads=(xspB[k],), writes=tuple(xB[k]))
            for j in range(KD):
                po, pob = fetch(wout[l, j], 2048)
                for tt in range(NT):
                    b = next_bank()
                    P.mm([(ps[b][:, 0:TT], [(po[:, k * 128:(k + 1) * 128], MG[:, k, tsl(tt)]) for k in range(KD)])],
                         bankB[b], reads=[pob] + [mgB[k][tt] for k in range(KD)])
                    P.op("dve", (lambda b=b, j=j, tt=tt: nc.vector.tensor_tensor(
                        out=Xf[:, j, tsl(tt)], in0=ps[b][:, 0:TT], in1=Xf[:, j, tsl(tt)], op=ALU.add)),
                        reads=(bankB[b],), writes=(xB[j][tt],))
            return [b for r_ in mgB for b in r_]

        for k in range(KD):
            P.dma("sp", (lambda k=k: nc.sync.dma_start(out=Xf[:, k, :], in_=xin[p][:, k * T:(k + 1) * T])),
                  xk_sem[k], writes=tuple(xB[k]))
        if HALO:
            P.dma("sp", (lambda: nc.sync.dma_start(out=MASK[:, 0:1], in_=maskd)), mask_sem, writes=(maskB,))
        m_old = []
        for l in range(n_layers):
            m_old = ffn(l, 0, m_old, l == 0, (lambda tt, l=l: norm_tile(tt, l, O_MN)))
            m_old = mixer(l, m_old)
            if l + 1 < n_layers:
                nxt = (lambda tt, l=l: norm_tile(tt, l + 1, O_F1N))
            else:
                nxt = (lambda tt: norm_tile(tt, None, None, make_h=False))
            m_old = ffn(l, 1, m_old, True, nxt)
        outB = [newbuf() for _ in range(2)]
        handoff(allh, outB)
        for k in range(KD):
            r = k % 2
            P.op("dve", (lambda k=k, r=r: nc.vector.scalar_tensor_tensor(
                out=OUTR[r], in0=Xf[:, k, HALO:T], scalar=FIN[:, k:k + 1], in1=RSTD[:, HALO:T], op0=ALU.mult, op1=ALU.mult)),
                reads=tuple(xB[k]) + tuple(rstdB) + (constB,), writes=(outB[r],))
            P.dma("sp", (lambda k=k, r=r: nc.sync.dma_start(out=outd[p, k], in_=OUTR[r])), out_sem[r],
                  reads=(outB[r],))

    def barrier_tokens():
        toks = [(esem[e], P.cnt[e]) for e in Prog.ENG if P.cnt[e] > 0]
        for sm in ring_sem + xk_sem + out_sem + [bc_sem, mask_sem]:
            c = P.dcnt.get(id(sm), 0)
            if c:
                toks.append((sm, c))
        return toks

    barrier = []
    for p in range(n_passes):
        run_pass(p, barrier)
        barrier = barrier_tokens()
    for r in range(2):
        P.wait("sp", (out_sem[r], P.dcnt.get(id(out_sem[r]), 0)))

    with nc.Block() as block:
        @block.tensor
        def _(e):
            P.replay("pe", e)

        @block.scalar
        def _(e):
            P.replay("act", e)

        @block.vector
        def _(e):
            P.replay("dve", e)

        @block.gpsimd
        def _(e):
            P.replay("pool", e)

        @block.sync
        def _(e):
            P.replay("sp", e)
    es.close()
    return nc, P


def _pieces(w, kin):
    Lw, K, F = w.shape
    kc = K // 128
    a = w.reshape(Lw, kc, 128, F // 128, 128)
    a = np.transpose(a, (0, 3, 2, 1, 4))
    return np.ascontiguousarray(a).reshape(Lw, F // 128, 128, kc * 128)


def prep_shared(inp):
    f = lambda a: np.asarray(a, dtype=np.float32)
    sh = {}
    sh["f1w1"] = _pieces(f(inp["ffn1_w1"]), D)
    sh["f1w3"] = _pieces(f(inp["ffn1_w3"]), D)
    sh["f1w2"] = np.ascontiguousarray(f(inp["ffn1_w2"]))
    sh["f2w1"] = _pieces(f(inp["ffn2_w1"]), D)
    sh["f2w3"] = _pieces(f(inp["ffn2_w3"]), D)
    sh["f2w2"] = np.ascontiguousarray(f(inp["ffn2_w2"]))
    sh["win"] = _pieces(f(inp["w_in"]), D)
    wb = f(inp["w_branch"])
    sh["wbr"] = _pieces(wb.reshape(L * 3, 1024, D), 1024).reshape(L, 48, 128, 1024)
    sh["wout"] = _pieces(f(inp["w_out"]), D)

    def cols(v, n):
        return np.ascontiguousarray(v.reshape(n, 128).T)

    small = np.zeros((128, L * NS), np.float32)
    bc = np.zeros((L, 128, 4096), np.float32)
    for l in range(L):
        s = small[:, l * NS:(l + 1) * NS]
        s[:, O_F1N:O_F1N + 16] = cols(f(inp["ffn1_norm"])[l], 16)
        s[:, O_MN:O_MN + 16] = cols(f(inp["mix_norm"])[l], 16)
        s[:, O_F2N:O_F2N + 16] = cols(f(inp["ffn2_norm"])[l], 16)
        bg = f(inp["b_gate"])[l]
        for i in range(3):
            s[:, O_BG + i * 16:O_BG + (i + 1) * 16] = cols(bg[i], 16)
        scw = f(inp["sconv_w"])[l]
        s[:, O_SCW:O_SCW + 24] = np.transpose(scw.reshape(3, 8, 128), (2, 1, 0)).reshape(128, 24)
        ccw = f(inp["conf_conv_w"])[l]
        s[:, O_CCW:O_CCW + 248] = np.transpose(ccw.reshape(31, 8, 128), (2, 1, 0)).reshape(128, 248)
        s[:, O_CCB:O_CCB + 8] = cols(f(inp["conf_conv_b"])[l], 8)
        s[:, O_CLG:O_CLG + 8] = cols(f(inp["conf_ln_g"])[l], 8)
        s[:, O_CLB:O_CLB + 8] = cols(f(inp["conf_ln_b"])[l], 8)
        bc[l, :, 0:1024] = f(inp["gmlp_ln_g"])[l][None, :]
        bc[l, :, 1024:2048] = f(inp["gmlp_ln_b"])[l][None, :]
        bc[l, :, 2048:3072] = f(inp["gmlp_b_s"])[l].reshape(1, 1024)
        ws = f(inp["gmlp_w_s"])[l]
        bc[l, :, 3072:4096] = np.transpose(ws, (2, 0, 1)).reshape(128, 1024)
    sh["small"] = small
    sh["bc"] = bc
    sh["fin"] = cols(f(inp["final_norm"]), 16)
    sh["ident"] = np.eye(128, dtype=np.float32)
    return sh


def prep_x(x, c):
    b, s0 = c // 4, (c % 4) * 2 * TOUT
    res = {}
    for p, (T, NT, TT, HALO) in enumerate(GEO):
        start = s0 + p * TOUT - HALO
        buf = np.zeros((T, D), np.float32)
        lo = max(start, 0)
        buf[lo - start:] = x[b, lo:start + T]
        res[f"xin{p}"] = np.ascontiguousarray(np.transpose(buf.reshape(T, KD, 128), (2, 1, 0))).reshape(128, KD * T)
    res["mask"] = np.full((128, 1), 0.0 if s0 == 0 else 1.0, np.float32)
    return res


_CACHE = {}


def kernel(**inputs):
    x = np.asarray(inputs["x"], dtype=np.float32)
    sh = prep_shared(inputs)
    if "nc" not in _CACHE:
        _CACHE["nc"] = build_program(L, 2)[0]
    nc = _CACHE["nc"]
    in_maps = []
    for c in range(NCORES):
        m = dict(sh)
        m.update(prep_x(x, c))
        in_maps.append(m)
    res = run_bass_kernel_spmd(nc, in_maps, core_ids=list(range(NCORES)))
    out = np.empty((2, 8192, D), np.float32)
    for c in range(NCORES):
        o = np.asarray(res.results[c]["out"])
        b, s0 = c // 4, (c % 4) * 2 * TOUT
        for p in range(2):
            out[b, s0 + p * TOUT:s0 + (p + 1) * TOUT, :] = np.transpose(o[p], (2, 0, 1)).reshape(TOUT, D)
    return out
```

```python
import numpy as np
import concourse.bass as bass
import concourse.mybir as mybir
from concourse.bass_utils import run_bass_kernel_spmd

F32 = mybir.dt.float32
BF16 = mybir.dt.bfloat16
AF = mybir.ActivationFunctionType
ALU = mybir.AluOpType

D = 2048
KD = 16
DFF = 5632
NFC = 44
NGRP = 11
TOUT = 1024
GEO = [(1280, 5, 256, 256), (1024, 4, 256, 0)]
TMAX = 1280
L = 4
NSLOT = 6
NCORES = 8
EPS = 1e-6

O_F1N, O_MN, O_F2N, O_BG, O_SCW, O_CCW, O_CCB, O_CLG, O_CLB = 0, 16, 32, 48, 96, 120, 368, 376, 384
NS = 392


class Buf:
    __slots__ = ("w", "r")

    def __init__(self):
        self.w = None
        self.r = {}

    def toks(self):
        out = list(self.r.values())
        if self.w is not None:
            out.append(self.w)
        return out


class Prog:
    ENG = ("pe", "act", "dve", "pool", "sp")

    def __init__(self, nc, sems):
        self.nc = nc
        self.q = {e: [] for e in self.ENG}
        self.sem = sems
        self.cnt = {e: 0 for e in self.ENG}
        self.seen = {e: {} for e in self.ENG}
        self.dcnt = {}

    def wait(self, eng, tok):
        if tok is None:
            return
        sem, val = tok
        key = id(sem)
        if self.seen[eng].get(key, 0) >= val:
            return
        self.seen[eng][key] = val
        self.q[eng].append(("w", sem, val))

    def _deps(self, eng, reads, writes, extra):
        for b in reads:
            self.wait(eng, b.w)
        for b in writes:
            self.wait(eng, b.w)
            for t in b.r.values():
                self.wait(eng, t)
        for t in extra:
            self.wait(eng, t)

    def _note(self, tok, reads, writes):
        key = id(tok[0])
        for b in reads:
            old = b.r.get(key)
            if old is None or old[1] < tok[1]:
                b.r[key] = tok
        for b in writes:
            b.w = tok
            b.r = {}

    def op(self, eng, fn, reads=(), writes=(), extra=()):
        self._deps(eng, reads, writes, extra)
        self.cnt[eng] += 1
        tok = (self.sem[eng], self.cnt[eng])
        self.q[eng].append(("i", fn, 1))
        self._note(tok, reads, writes)
        return tok

    def mm(self, groups, bank, reads):
        self._deps("pe", reads, (bank,), ())
        flat = []
        for out_ap, mms in groups:
            n = len(mms)
            for i, (lt, rh) in enumerate(mms):
                flat.append((out_ap, lt, rh, i == 0, i == n - 1))
        self.cnt["pe"] += 1
        tok = (self.sem["pe"], self.cnt["pe"])
        nc = self.nc
        for idx, (o, lt, rh, st, sp) in enumerate(flat):
            last = idx == len(flat) - 1
            self.q["pe"].append(("i", (lambda o=o, lt=lt, rh=rh, st=st, sp=sp:
                                       nc.tensor.matmul(o, lhsT=lt, rhs=rh, start=st, stop=sp)), 1 if last else 0))
        self._note(tok, reads, (bank,))
        return tok

    def dma(self, eng, fn, sem, reads=(), writes=(), extra=()):
        self._deps(eng, reads, writes, extra)
        self.dcnt[id(sem)] = self.dcnt.get(id(sem), 0) + 16
        tok = (sem, self.dcnt[id(sem)])
        self.q[eng].append(("d", fn, sem))
        self._note(tok, reads, writes)
        return tok

    def replay(self, eng, engobj):
        sem = self.sem[eng]
        for item in self.q[eng]:
            if item[0] == "w":
                engobj.wait_ge(item[1], item[2])
            elif item[0] == "i":
                ins = item[1]()
                if item[2]:
                    ins.then_inc(sem, 1)
            else:
                item[1]().then_inc(item[2], 16)


def handoff(old_bufs, new_bufs):
    merged = {}
    for b in old_bufs:
        for t in b.toks():
            k = id(t[0])
            if k not in merged or merged[k][1] < t[1]:
                merged[k] = t
    for nb in new_bufs:
        for k, t in merged.items():
            o = nb.r.get(k)
            if o is None or o[1] < t[1]:
                nb.r[k] = t


def build_program(n_layers=L, n_passes=2):
    nc = bass.Bass("TRN2", target_bir_lowering=False)

    def din(name, shape):
        return nc.dram_tensor(name, shape, F32, kind="ExternalInput").ap()

    xin = [din(f"xin{p}", [128, KD * GEO[p][0]]) for p in range(n_passes)]
    maskd = din("mask", [128, 1])
    fw1 = [din("f1w1", [L, NFC, 128, 2048]), din("f2w1", [L, NFC, 128, 2048])]
    fw3 = [din("f1w3", [L, NFC, 128, 2048]), din("f2w3", [L, NFC, 128, 2048])]
    fw2 = [din("f1w2", [L, DFF, D]), din("f2w2", [L, DFF, D])]
    win = din("win", [L, 104, 128, 2048])
    wbr = din("wbr", [L, 48, 128, 1024])
    wout = din("wout", [L, 16, 128, 2048])
    smalld = din("small", [128, L * NS])
    bcd = din("bc", [L, 128, 4096])
    find = din("fin", [128, 16])
    identd = din("ident", [128, 128])
    xsp = nc.dram_tensor("xspill", [KD, 128, TMAX], F32, kind="Internal").ap()
    outd = nc.dram_tensor("out", [n_passes, KD, 128, TOUT], F32, kind="ExternalOutput").ap()

    W_X, W_H, W_M = KD * TMAX, KD * TMAX // 2, KD * TMAX // 2
    sizes = dict(X=W_X, H=W_H, M=W_M, RING=NSLOT * 1024, SMALL=L * NS, FIN=16, WST=512, RSTD=TMAX, ONES=64, IDENT=64,
                 MASK=2, SQT=512, SA=512, STA=10 * 12, MVA=20, RSA=20, TAILZ=L * 8 * 15, TAILS=L * 8 * 2)
    NBIG = sum(sizes.values())

    import contextlib
    es = contextlib.ExitStack()
    big = es.enter_context(nc.sbuf_tensor("big", [128, NBIG], F32))
    ps = [es.enter_context(nc.psum_tensor(f"ps{i}", [128, 512], F32)) for i in range(8)]
    esem = {e: es.enter_context(nc.semaphore(f"s_{e}")) for e in Prog.ENG}
    ring_sem = [es.enter_context(nc.semaphore(f"ring{i}")) for i in range(NSLOT)]
    xk_sem = [es.enter_context(nc.semaphore(f"xk{i}")) for i in range(KD)]
    out_sem = [es.enter_context(nc.semaphore(f"o{i}")) for i in range(2)]
    setup_sem = es.enter_context(nc.semaphore("setup"))
    bc_sem = es.enter_context(nc.semaphore("bcs"))
    mask_sem = es.enter_context(nc.semaphore("masks"))

    off = {}
    o = 0
    for k_, v_ in sizes.items():
        off[k_] = o
        o += v_

    def reg(name, a=0, n=None):
        n = sizes[name] - a if n is None else n
        return big[:, off[name] + a: off[name] + a + n]

    ringv = [reg("RING", s * 1024, 1024).bitcast(BF16) for s in range(NSLOT)]
    SMALL = reg("SMALL")
    FIN = reg("FIN")
    WST = reg("WST").bitcast(BF16)
    RSTD = reg("RSTD")
    ONES = reg("ONES").bitcast(BF16)
    IDENT = reg("IDENT").bitcast(BF16)
    MASK = reg("MASK")
    EPSC = MASK[:, 1:2]
    SQT = reg("SQT")
    SA = reg("SA")
    RSA = reg("RSA")
    TAILZ = reg("TAILZ").bitcast(BF16).rearrange("p (l j t) -> p l j t", l=L, j=8)
    TAILS = reg("TAILS").rearrange("p (l j t) -> p l j t", l=L, j=8)

    P = Prog(nc, esem)

    bankB = [Buf() for _ in range(8)]
    slotB = [Buf() for _ in range(NSLOT)]
    constB, epsB, maskB, wstB, statB, onesB = Buf(), Buf(), Buf(), Buf(), Buf(), Buf()
    sqtB, saB = Buf(), Buf()
    xspB = [Buf() for _ in range(KD)]
    tailzB = [[Buf() for _ in range(8)] for _ in range(L)]
    tailsB = [[Buf() for _ in range(8)] for _ in range(L)]
    st = dict(bank=0, piece=0)

    def next_bank():
        b = st["bank"] % 8
        st["bank"] += 1
        return b

    def fetch(dram2d, n):
        s = st["piece"] % NSLOT
        st["piece"] += 1
        dst = ringv[s][:, 0:n]
        P.dma("pool", (lambda dst=dst, src=dram2d: nc.gpsimd.dma_start(out=dst, in_=src)),
              ring_sem[s], writes=(slotB[s],))
        return ringv[s], slotB[s]

    def scol(l, o_, n=1):
        return SMALL[:, l * NS + o_: l * NS + o_ + n]

    P.dma("sp", lambda: nc.sync.dma_start(out=SMALL, in_=smalld), setup_sem, writes=(constB,))
    P.dma("sp", lambda: nc.sync.dma_start(out=FIN, in_=find), setup_sem, writes=(constB,))
    P.dma("pool", lambda: nc.gpsimd.dma_start(out=IDENT, in_=identd), setup_sem, writes=(constB,))
    constB.w = (setup_sem, 48)
    P.op("dve", lambda: nc.vector.memset(ONES, 1.0), writes=(onesB,))
    P.op("dve", lambda: nc.vector.memset(EPSC, EPS), writes=(epsB,))

    def run_pass(p, barrier):
        T, NT, TT, HALO = GEO[p]
        NBLK = T // 128
        first = (p == 0)

        def newbuf():
            b = Buf()
            for t in barrier:
                b.r[id(t[0])] = t
            return b

        def tsl(tt):
            return slice(tt * TT, (tt + 1) * TT)

        def tiles_of(a, b):
            return list(range(a // TT, (b - 1) // TT + 1))

        Xf = big[:, off["X"]: off["X"] + KD * T].rearrange("p (k t) -> p k t", k=KD)
        Hb = big[:, off["H"]: off["H"] + KD * T // 2].bitcast(BF16).rearrange("p (k t) -> p k t", k=KD)
        Mreg = big[:, off["M"]: off["M"] + KD * T // 2]
        SLOTW = 4 * T

        def xslot(i, a=0, n=SLOTW):
            return big[:, off["X"] + i * SLOTW + a: off["X"] + i * SLOTW + a + n]

        YA = xslot(0).bitcast(BF16).rearrange("p (k t) -> p k t", k=8)
        YB = xslot(1).bitcast(BF16).rearrange("p (k t) -> p k t", k=8)
        YC = xslot(2).bitcast(BF16).rearrange("p (k t) -> p k t", k=8)
        BCT = xslot(2, 0, 4096)
        TMPS = xslot(1, 0, 512)
        VLN = xslot(3).bitcast(BF16).rearrange("p (n f) -> p n f", n=NBLK)
        ZBW = (T + 32) // 2
        SGC = xslot(3, 0, T)
        ZB = [xslot(3, T + i * ZBW, ZBW).bitcast(BF16) for i in range(2)]
        DIAG = xslot(3, T + 2 * ZBW, 1984).bitcast(BF16)
        SS = xslot(3, 0, T + 2)
        CS = xslot(3, T + 2, T)
        CG = CS
        WC = TT
        assert 2 * T + 2 <= SLOTW - 1024 and WC == 256
        ZCB = big[:, off["SQT"]: off["SQT"] + 1024].bitcast(BF16).rearrange("p (j t) -> p j t", j=8)
        ZSQ = xslot(3, SLOTW - 1024, 1024).bitcast(BF16).rearrange("p (j t) -> p j t", j=8)
        LT = [big[:, off["RSTD"] + i * WC: off["RSTD"] + (i + 1) * WC] for i in range(5)]
        SGM = [xslot(3, i * (T // 2), T // 2).bitcast(BF16) for i in range(2)]
        MF = xslot(3, T, T)
        MT = xslot(3, 2 * T, T)
        GR = Mreg[:, 0:4 * T].bitcast(BF16).rearrange("p (s c t) -> p s c t", s=2, c=4)
        XSQ2 = [big[:, off["M"] + 4 * T + r_ * 2048: off["M"] + 4 * T + (r_ + 1) * 2048].bitcast(BF16)
                .rearrange("p (k t) -> p k t", k=KD) for r_ in range(2)]
        assert TT == 256 and 4 * T + 4096 <= KD * T // 2
        VG = Mreg.rearrange("p (n f) -> p n f", n=NBLK)
        ZC = Mreg.rearrange("p (j t) -> p j t", j=8)
        MG = Mreg.bitcast(BF16).rearrange("p (k t) -> p k t", k=KD)
        OUTR = [big[:, off["H"] + i * TOUT: off["H"] + (i + 1) * TOUT] for i in range(2)]
        STA = reg("STA", 0, NBLK * 12).rearrange("p (n s) -> p n s", n=NBLK)
        MVA = reg("MVA", 0, NBLK * 2).rearrange("p (n s) -> p n s", n=NBLK)

        xB = [[newbuf() for _ in range(NT)] for _ in range(KD)]
        hB = [[newbuf() for _ in range(NT)] for _ in range(KD)]
        rstdB = [newbuf() for _ in range(NT)]
        xsqB2 = [[newbuf() for _ in range(KD)] for _ in range(2)]
        xsqB = [b for r_ in xsqB2 for b in r_]
        nst = dict(n=0)
        allx = [b for r_ in xB for b in r_]
        allh = [b for r_ in hB for b in r_]

        def proj_ft(piece, pbuf, tt, nk=KD):
            b = next_bank()
            P.mm([(ps[b][:, 0:TT], [(piece[:, k * 128:(k + 1) * 128], Hb[:, k, tsl(tt)]) for k in range(nk)])],
                 bankB[b], reads=[pbuf] + [hB[k][tt] for k in range(nk)])
            return b

        SQT2 = [SQT[:, i_ * 256:(i_ + 1) * 256] for i_ in range(2)]
        sqtB2 = [newbuf() for _ in range(2)]

        def norm_p1(tt):
            r = nst["n"] % 2
            nst["n"] += 1
            XS = XSQ2[r]
            for k in range(KD):
                P.op("act", (lambda k=k, tt=tt, XS=XS: nc.scalar.activation(out=XS[:, k, :], in_=Xf[:, k, tsl(tt)],
                                                                           func=AF.Square)),
                     reads=(xB[k][tt],), writes=(xsqB2[r][k],))
            b = next_bank()
            P.mm([(ps[b][:, 0:TT], [(ONES, XS[:, k, :]) for k in range(KD)])], bankB[b], reads=xsqB2[r] + [onesB])
            P.op("act", (lambda b=b, r=r: nc.scalar.activation(out=SQT2[r], in_=ps[b][:, 0:TT], func=AF.Sqrt,
                                                               bias=EPSC, scale=1.0 / D)),
                 reads=(bankB[b], epsB), writes=(sqtB2[r],))
            return r

        def norm_p2(tt, r, l, ocol, make_h):
            P.op("dve", (lambda tt=tt, r=r: nc.vector.reciprocal(out=RSTD[:, tsl(tt)], in_=SQT2[r])),
                 reads=(sqtB2[r],), writes=(rstdB[tt],))
            if make_h:
                for k in range(KD):
                    P.op("dve", (lambda k=k, tt=tt: nc.vector.scalar_tensor_tensor(
                        out=Hb[:, k, tsl(tt)], in0=Xf[:, k, tsl(tt)], scalar=scol(l, ocol + k),
                        in1=RSTD[:, tsl(tt)], op0=ALU.mult, op1=ALU.mult)),
                        reads=(xB[k][tt], rstdB[tt], constB), writes=(hB[k][tt],))

        def norm_tile(tt, l, ocol, make_h=True):
            norm_p2(tt, norm_p1(tt), l, ocol, make_h)

        def ffn(l, which, m_old_bufs, do_norm, nxt):
            ocol = O_F1N if which == 0 else O_F2N
            if do_norm:
                handoff(m_old_bufs, xsqB)
                for tt in range(NT):
                    norm_tile(tt, l, ocol)
            gB = [[[newbuf() for _ in range(NT)] for _ in range(4)] for _ in range(2)]
            gall = [b for s_ in gB for c_ in s_ for b in c_]
            handoff(m_old_bufs, gall)
            w1, w3, w2 = fw1[which], fw3[which], fw2[which]

            def p1(G):
                s = G % 2
                for ci in range(4):
                    c = G * 4 + ci
                    pa, pab = fetch(w1[l, c], 2048)
                    pb, pbb = fetch(w3[l, c], 2048)
                    for tt in range(NT):
                        ba = proj_ft(pa, pab, tt)
                        bb = proj_ft(pb, pbb, tt)
                        P.op("act", (lambda ba=ba: nc.scalar.activation(out=SA[:, 0:TT], in_=ps[ba][:, 0:TT], func=AF.Silu)),
                             reads=(bankB[ba],), writes=(saB,))
                        P.op("dve", (lambda bb=bb, s=s, ci=ci, tt=tt: nc.vector.tensor_tensor(
                            out=GR[:, s, ci, tsl(tt)], in0=ps[bb][:, 0:TT], in1=SA[:, 0:TT], op=ALU.mult)),
                            reads=(bankB[bb], saB), writes=(gB[s][ci][tt],))

            def p2(G, last):
                s = G % 2
                pw = [fetch(w2[l, (G * 4 + ci) * 128:(G * 4 + ci + 1) * 128, :], 2048) for ci in range(4)]
                order = ([(j, tt) for tt in range(NT) for j in range(KD)] if last
                         else [(j, tt) for j in range(KD) for tt in range(NT)])
                pending = None
                for (j, tt) in order:
                    if True:
                        b = next_bank()
                        P.mm([(ps[b][:, 0:TT], [(pw[ci][0][:, j * 128:(j + 1) * 128], GR[:, s, ci, tsl(tt)])
                                                for ci in range(4)])],
                             bankB[b], reads=[pw[ci][1] for ci in range(4)] + [gB[s][ci][tt] for ci in range(4)])
                        P.op("dve", (lambda b=b, j=j, tt=tt: nc.vector.scalar_tensor_tensor(
                            out=Xf[:, j, tsl(tt)], in0=ps[b][:, 0:TT], scalar=0.5, in1=Xf[:, j, tsl(tt)],
                            op0=ALU.mult, op1=ALU.add)),
                            reads=(bankB[b],), writes=(xB[j][tt],))
                    if last and j == KD - 1 and nxt is not None:
                        r_ = norm_p1(tt)
                        if pending is not None:
                            norm_p2(*pending)
                        pending = (tt, r_) + nxt
                if pending is not None:
                    norm_p2(*pending)

            for G in range(NGRP + 1):
                if G < NGRP:
                    p1(G)
                if G >= 1:
                    p2(G - 1, G - 1 == NGRP - 1)
            return gall + xsqB

        def mixer(l, m_old_bufs):
            xs = m_old_bufs
            for k in range(KD):
                P.dma("sp", (lambda k=k: nc.sync.dma_start(out=xsp[k][:, 0:T], in_=Xf[:, k, :])), xk_sem[k],
                      reads=tuple(xB[k]), writes=(xspB[k],))
            yaB = [[newbuf() for _ in range(NT)] for _ in range(8)]
            ybB = [[newbuf() for _ in range(NT)] for _ in range(8)]
            ycB = [[newbuf() for _ in range(NT)] for _ in range(8)]
            bcB, tmpsB = newbuf(), newbuf()
            vlnB = [newbuf() for _ in range(NBLK)]
            handoff(allx, [b for r_ in yaB for b in r_] + [bcB] + vlnB + [tmpsB])
            P.dma("sp", (lambda: nc.sync.dma_start(out=BCT, in_=bcd[l])), bc_sem, writes=(bcB,))
            P.op("dve", lambda: nc.vector.tensor_copy(out=WST, in_=BCT[:, 3072:4096]), reads=(bcB,), writes=(wstB,))
            wst3 = WST.rearrange("p (g i) -> p g i", g=8)
            P.op("dve", lambda: nc.vector.memset(wst3[64:128, :, 0:64], 0.0), writes=(wstB,))

            vgrp = [list(range(a, min(a + 4, NBLK))) for a in range(0, NBLK, 4)]
            vgB = [[newbuf() for _ in range(8)] for _ in vgrp]
            handoff(xs, [b for r_ in vgB for b in r_])
            for c in range(8):
                pv, pvb = fetch(win[l, 8 + c], 2048)
                for gi, blks in enumerate(vgrp):
                    b = next_bank()
                    groups = []
                    for bi, n in enumerate(blks):
                        groups.append((ps[b][:, bi * 128:(bi + 1) * 128],
                                       [(Hb[:, k, n * 128:(n + 1) * 128], pv[:, k * 128:(k + 1) * 128]) for k in range(KD)]))
                    tts = tiles_of(blks[0] * 128, (blks[-1] + 1) * 128)
                    P.mm(groups, bankB[b], reads=[pvb] + [hB[k][t_] for k in range(KD) for t_ in tts])
                    nb = len(blks)
                    P.op("act", (lambda b=b, n0=blks[0], nb=nb, c=c: nc.scalar.activation(
                        out=VG[:, n0:n0 + nb, c * 128:(c + 1) * 128],
                        in_=ps[b][:, 0:nb * 128].rearrange("p (n f) -> p n f", n=nb), func=AF.Gelu)),
                        reads=(bankB[b],), writes=(vgB[gi][c],))
            for n in range(NBLK):
                rd = tuple(vgB[n // 4])
                P.op("dve", (lambda n=n: nc.vector.bn_stats(out=STA[:, n, 0:6], in_=VG[:, n, 0:512])), reads=rd, writes=(statB,))
                P.op("dve", (lambda n=n: nc.vector.bn_stats(out=STA[:, n, 6:12], in_=VG[:, n, 512:1024])), reads=rd, writes=(statB,))
                P.op("dve", (lambda n=n: nc.vector.bn_aggr(out=MVA[:, n, :], in_=STA[:, n, :])), reads=(statB,), writes=(statB,))
            P.op("act", (lambda: nc.scalar.activation(out=RSA[:, 0:NBLK], in_=MVA[:, :, 1], func=AF.Sqrt, bias=EPSC, scale=1.0)),
                 reads=(statB, epsB), writes=(statB,))
            P.op("dve", (lambda: nc.vector.reciprocal(out=RSA[:, 10:10 + NBLK], in_=RSA[:, 0:NBLK])), reads=(statB,), writes=(statB,))
            for c in range(8):
                pu, pub = fetch(win[l, c], 2048)
                for tt in range(NT):
                    b = proj_ft(pu, pub, tt)
                    P.op("act", (lambda b=b, c=c, tt=tt: nc.scalar.activation(out=YA[:, c, tsl(tt)], in_=ps[b][:, 0:TT], func=AF.Gelu)),
                         reads=(bankB[b],), writes=(yaB[c][tt],))
            for n in range(NBLK):
                rd = tuple(vgB[n // 4])
                P.op("dve", (lambda n=n: nc.vector.tensor_scalar(out=VG[:, n, :], in0=VG[:, n, :], scalar1=MVA[:, n, 0:1],
                                                                 scalar2=RSA[:, 10 + n:10 + n + 1],
                                                                 op0=ALU.subtract, op1=ALU.mult)),
                     reads=(statB,), writes=rd)
                P.op("dve", (lambda n=n: nc.vector.tensor_tensor(out=VG[:, n, :], in0=VG[:, n, :], in1=BCT[:, 0:1024], op=ALU.mult)),
                     reads=(bcB,), writes=rd)
                P.op("dve", (lambda n=n: nc.vector.tensor_tensor(out=VLN[:, n, :], in0=VG[:, n, :], in1=BCT[:, 1024:2048], op=ALU.add)),
                     reads=rd + (bcB,), writes=(vlnB[n],))
            for n in range(NBLK):
                tts = tiles_of(n * 128, (n + 1) * 128)
                for half in range(2):
                    b = next_bank()
                    groups = []
                    for gi in range(4):
                        g = half * 4 + gi
                        groups.append((ps[b][:, gi * 128:(gi + 1) * 128],
                                       [(VLN[:, n, g * 128:(g + 1) * 128], WST[:, g * 128:(g + 1) * 128])]))
                    P.mm(groups, bankB[b], reads=[vlnB[n], wstB])
                    P.op("dve", (lambda b=b, half=half: nc.vector.tensor_tensor(
                        out=TMPS, in0=ps[b][:, 0:512], in1=BCT[:, 2048 + half * 512: 2048 + (half + 1) * 512], op=ALU.add)),
                        reads=(bankB[b], bcB), writes=(tmpsB,))
                    P.op("dve", (lambda half=half, n=n: nc.vector.tensor_tensor(
                        out=YA[:, half * 4:(half + 1) * 4, n * 128:(n + 1) * 128],
                        in0=TMPS.rearrange("p (g i) -> p g i", g=4),
                        in1=YA[:, half * 4:(half + 1) * 4, n * 128:(n + 1) * 128], op=ALU.mult)),
                        reads=(tmpsB,), writes=tuple(yaB[half * 4 + gi][t_] for gi in range(4) for t_ in tts))

            zcB = [[newbuf() for _ in range(NT)] for _ in range(8)]
            handoff([b for r_ in vgB for b in r_], [b for r_ in zcB for b in r_])
            sgB = [newbuf() for _ in range(NT)]
            zbB = [newbuf() for _ in range(2)]
            diagB = newbuf()
            handoff(vlnB, sgB + zbB + [diagB])
            pend = None

            def conv_c(j, r):
                for tt in range(NT):
                    b = next_bank()
                    P.mm([(ps[b][:, 0:TT], [(DIAG[:, kk * 128:(kk + 1) * 128], ZB[r][:, kk + tt * TT: kk + tt * TT + TT])
                                            for kk in range(31)])], bankB[b], reads=[diagB, zbB[r]])
                    P.op("act", (lambda b=b, j=j, tt=tt: nc.scalar.activation(
                        out=ZC[:, j, tsl(tt)], in_=ps[b][:, 0:TT], func=AF.Identity, bias=scol(l, O_CCB + j), scale=1.0)),
                        reads=(bankB[b], constB), writes=(zcB[j][tt],))

            for j in range(8):
                r = j % 2
                pg, pgb = fetch(win[l, 48 + j], 2048)
                pvv, pvvb = fetch(win[l, 40 + j], 2048)
                for tt in range(NT):
                    b = proj_ft(pg, pgb, tt)
                    P.op("act", (lambda b=b, tt=tt: nc.scalar.activation(out=SGC[:, tsl(tt)], in_=ps[b][:, 0:TT], func=AF.Sigmoid)),
                         reads=(bankB[b],), writes=(sgB[tt],))
                if first:
                    P.op("dve", (lambda r=r: nc.vector.memset(ZB[r][:, 0:30], 0.0)), writes=(zbB[r],))
                else:
                    P.op("dve", (lambda r=r, j=j: nc.vector.tensor_copy(out=ZB[r][:, 0:30], in_=TAILZ[:, l, j, :])),
                         reads=(tailzB[l][j],), writes=(zbB[r],))
                for tt in range(NT):
                    b = proj_ft(pvv, pvvb, tt)
                    P.op("dve", (lambda b=b, tt=tt, r=r: nc.vector.tensor_tensor(
                        out=ZB[r][:, 30 + tt * TT: 30 + (tt + 1) * TT], in0=ps[b][:, 0:TT], in1=SGC[:, tsl(tt)], op=ALU.mult)),
                        reads=(bankB[b], sgB[tt]), writes=(zbB[r],))
                if HALO:
                    P.op("dve", (lambda r=r: nc.vector.tensor_scalar(out=ZB[r][:, 30:30 + HALO], in0=ZB[r][:, 30:30 + HALO],
                                                                     scalar1=MASK[:, 0:1], scalar2=None, op0=ALU.mult)),
                         reads=(maskB,), writes=(zbB[r],))
                if first and n_passes > 1:
                    P.op("dve", (lambda r=r, j=j: nc.vector.tensor_copy(out=TAILZ[:, l, j, :], in_=ZB[r][:, T:T + 30])),
                         reads=(zbB[r],), writes=(tailzB[l][j],))
                if pend is not None:
                    conv_c(*pend)
                for kk in range(31):
                    P.op("act", (lambda kk=kk, j=j: nc.scalar.activation(out=DIAG[:, kk * 128:(kk + 1) * 128], in_=IDENT, func=AF.Copy,
                                                                        scale=scol(l, O_CCW + j * 31 + kk))),
                         reads=(constB,), writes=(diagB,))
                pend = (j, r)
            conv_c(*pend)

            cgB = [newbuf() for _ in range(NT)]
            ssB, csB = newbuf(), newbuf()
            zcbB = [newbuf() for _ in range(8)]
            zsqB = [newbuf() for _ in range(8)]
            ltB = [newbuf() for _ in range(5)]
            handoff(sgB + zbB + [diagB], cgB + [ssB, csB] + zsqB)
            handoff(rstdB + sqtB2 + [saB], ltB + zcbB)
            handoff([tmpsB], [b for r_ in ybB for b in r_])
            handoff([bcB], [b for r_ in ycB for b in r_])

            def lnc_pre(q):
                qs = slice(q * WC, (q + 1) * WC)
                for j in range(8):
                    P.op("act", (lambda j=j, qs=qs: nc.scalar.activation(out=ZCB[:, j, :], in_=ZC[:, j, qs], func=AF.Copy)),
                         reads=(zcB[j][q],), writes=(zcbB[j],))
                    P.op("act", (lambda j=j, qs=qs: nc.scalar.activation(out=ZSQ[:, j, :], in_=ZC[:, j, qs], func=AF.Square)),
                         reads=(zcB[j][q],), writes=(zsqB[j],))

            def lnc_post(q):
                qs = slice(q * WC, (q + 1) * WC)
                mt = q
                b1 = next_bank()
                P.mm([(ps[b1][:, 0:WC], [(ONES, ZCB[:, j, :]) for j in range(8)])], bankB[b1], reads=zcbB + [onesB])
                b2 = next_bank()
                P.mm([(ps[b2][:, 0:WC], [(ONES, ZSQ[:, j, :]) for j in range(8)])], bankB[b2], reads=zsqB + [onesB])
                P.op("act", (lambda b1=b1: nc.scalar.activation(out=LT[0], in_=ps[b1][:, 0:WC], func=AF.Square, scale=1.0 / 1024)),
                     reads=(bankB[b1],), writes=(ltB[0],))
                P.op("dve", (lambda b2=b2: nc.vector.scalar_tensor_tensor(out=LT[1], in0=ps[b2][:, 0:WC], scalar=1.0 / 1024,
                                                                          in1=LT[0], op0=ALU.mult, op1=ALU.subtract)),
                     reads=(bankB[b2], ltB[0]), writes=(ltB[1],))
                P.op("act", (lambda: nc.scalar.activation(out=LT[0], in_=LT[1], func=AF.Sqrt, bias=EPSC, scale=1.0)),
                     reads=(ltB[1], epsB), writes=(ltB[0],))
                P.op("dve", (lambda: nc.vector.reciprocal(out=LT[4], in_=LT[0])), reads=(ltB[0],), writes=(ltB[4],))
                for j in range(8):
                    P.op("dve", (lambda b1=b1, j=j, qs=qs: nc.vector.scalar_tensor_tensor(
                        out=LT[2], in0=ps[b1][:, 0:WC], scalar=-1.0 / 1024, in1=ZC[:, j, qs], op0=ALU.mult, op1=ALU.add)),
                        reads=(bankB[b1], zcB[j][mt]), writes=(ltB[2],))
                    P.op("dve", (lambda: nc.vector.tensor_tensor(out=LT[3], in0=LT[2], in1=LT[4], op=ALU.mult)),
                         reads=(ltB[2], ltB[4]), writes=(ltB[3],))
                    P.op("act", (lambda j=j, qs=qs: nc.scalar.activation(out=YC[:, j, qs], in_=LT[3], func=AF.Silu,
                                                                        bias=scol(l, O_CLB + j), scale=scol(l, O_CLG + j))),
                         reads=(ltB[3], constB), writes=(ycB[j][mt],))

            for j in range(8):
                if j < NT:
                    lnc_pre(j)
                pc, pcb = fetch(win[l, 24 + j], 2048)
                px, pxb = fetch(win[l, 32 + j], 2048)
                pbg, pbgb = fetch(win[l, 16 + j], 2048)
                for tt in range(NT):
                    b = proj_ft(pc, pcb, tt)
                    P.op("act", (lambda b=b, tt=tt: nc.scalar.activation(out=CG[:, tsl(tt)], in_=ps[b][:, 0:TT], func=AF.Copy)),
                         reads=(bankB[b],), writes=(cgB[tt], csB))
                if first:
                    P.op("dve", (lambda: nc.vector.memset(SS[:, 0:2], 0.0)), writes=(ssB,))
                else:
                    P.op("dve", (lambda j=j: nc.vector.tensor_copy(out=SS[:, 0:2], in_=TAILS[:, l, j, :])),
                         reads=(tailsB[l][j],), writes=(ssB,))
                for tt in range(NT):
                    b = proj_ft(px, pxb, tt)
                    P.op("dve", (lambda b=b, tt=tt: nc.vector.tensor_tensor(
                        out=SS[:, 2 + tt * TT: 2 + (tt + 1) * TT], in0=ps[b][:, 0:TT], in1=CG[:, tsl(tt)], op=ALU.mult)),
                        reads=(bankB[b], cgB[tt]), writes=(ssB,))
                if HALO:
                    P.op("dve", (lambda: nc.vector.tensor_scalar(out=SS[:, 2:2 + HALO], in0=SS[:, 2:2 + HALO], scalar1=MASK[:, 0:1],
                                                                 scalar2=None, op0=ALU.mult)), reads=(maskB,), writes=(ssB,))
                if first and n_passes > 1:
                    P.op("dve", (lambda j=j: nc.vector.tensor_copy(out=TAILS[:, l, j, :], in_=SS[:, T:T + 2])),
                         reads=(ssB,), writes=(tailsB[l][j],))
                P.op("dve", (lambda j=j: nc.vector.tensor_scalar(out=CS, in0=SS[:, 0:T], scalar1=scol(l, O_SCW + j * 3 + 0),
                                                                 scalar2=None, op0=ALU.mult)), reads=(ssB, constB), writes=(csB,))
                for kk in (1, 2):
                    P.op("dve", (lambda j=j, kk=kk: nc.vector.scalar_tensor_tensor(
                        out=CS, in0=SS[:, kk:kk + T], scalar=scol(l, O_SCW + j * 3 + kk), in1=CS, op0=ALU.mult, op1=ALU.add)),
                        reads=(ssB, constB), writes=(csB,))
                for tt in range(NT):
                    b = proj_ft(pbg, pbgb, tt)
                    P.op("dve", (lambda b=b, tt=tt, j=j: nc.vector.tensor_tensor(
                        out=YB[:, j, tsl(tt)], in0=ps[b][:, 0:TT], in1=CS[:, tsl(tt)], op=ALU.mult)),
                        reads=(bankB[b], csB), writes=(ybB[j][tt],))
                if j < NT:
                    lnc_post(j)

            mgB = [[newbuf() for _ in range(NT)] for _ in range(KD)]
            handoff([b for r_ in zcB for b in r_], [b for r_ in mgB for b in r_])
            sgmB = [[newbuf() for _ in range(NT)] for _ in range(2)]
            mfB = [newbuf() for _ in range(NT)]
            mtB = [newbuf() for _ in range(NT)]
            handoff(cgB + [ssB, csB] + zsqB, [b for r_ in sgmB for b in r_] + mfB + mtB)
            handoff(ltB + zcbB, rstdB + sqtB2 + [saB])
            ysrc = [(YA, yaB), (YB, ybB), (YC, ycB)]
            cnt = 0
            for j in range(KD):
                for i in range(3):
                    r = cnt % 2
                    cnt += 1
                    pg, pgb = fetch(win[l, 56 + i * 16 + j], 2048)
                    pb, pbb = fetch(wbr[l, i * 16 + j], 1024)
                    for tt in range(NT):
                        b = proj_ft(pg, pgb, tt)
                        P.op("act", (lambda b=b, r=r, tt=tt, i=i, j=j: nc.scalar.activation(
                            out=SGM[r][:, tsl(tt)], in_=ps[b][:, 0:TT], func=AF.Sigmoid, bias=scol(l, O_BG + i * 16 + j), scale=1.0)),
                            reads=(bankB[b], constB), writes=(sgmB[r][tt],))
                    Y, yB_ = ysrc[i]
                    for tt in range(NT):
                        b = next_bank()
                        P.mm([(ps[b][:, 0:TT], [(pb[:, k * 128:(k + 1) * 128], Y[:, k, tsl(tt)]) for k in range(8)])],
                             bankB[b], reads=[pbb] + [yB_[k][tt] for k in range(8)])
                        if i == 0:
                            P.op("dve", (lambda b=b, r=r, tt=tt: nc.vector.tensor_tensor(
                                out=MF[:, tsl(tt)], in0=ps[b][:, 0:TT], in1=SGM[r][:, tsl(tt)], op=ALU.mult)),
                                reads=(bankB[b], sgmB[r][tt]), writes=(mfB[tt],))
                        else:
                            P.op("dve", (lambda b=b, r=r, tt=tt: nc.vector.tensor_tensor(
                                out=MT[:, tsl(tt)], in0=ps[b][:, 0:TT], in1=SGM[r][:, tsl(tt)], op=ALU.mult)),
                                reads=(bankB[b], sgmB[r][tt]), writes=(mtB[tt],))
                            if i == 1:
                                P.op("dve", (lambda tt=tt: nc.vector.tensor_tensor(
                                    out=MF[:, tsl(tt)], in0=MF[:, tsl(tt)], in1=MT[:, tsl(tt)], op=ALU.add)),
                                    reads=(mtB[tt],), writes=(mfB[tt],))
                            else:
                                P.op("dve", (lambda tt=tt, j=j: nc.vector.tensor_tensor(
                                    out=MG[:, j, tsl(tt)], in0=MF[:, tsl(tt)], in1=MT[:, tsl(tt)], op=ALU.add)),
                                    reads=(mtB[tt], mfB[tt]), writes=(mgB[j][tt],))

            xtemps = ([b for r_ in yaB for b in r_] + [b for r_ in ybB for b in r_] + [b for r_ in ycB for b in r_]
                      + [b for r_ in sgmB for b in r_] + mfB + mtB + [bcB, tmpsB])
            handoff(xtemps, allx)
            for k in range(KD):
                P.dma("sp", (lambda k=k: nc.sync.dma_start(out=Xf[:, k, :], in_=xsp[k][:, 0:T])), xk_sem[k],
                      reads=(xspB[k],), writes=tuple(xB[k]))
            for j in range(KD):
                po, pob = fetch(wout[l, j], 2048)
                for tt in range(NT):
                    b = next_bank()
                    P.mm([(ps[b][:, 0:TT], [(po[:, k * 128:(k + 1) * 128], MG[:, k, tsl(tt)]) for k in range(KD)])],
                         bankB[b], reads=[pob] + [mgB[k][tt] for k in range(KD)])
                    P.op("dve", (lambda b=b, j=j, tt=tt: nc.vector.tensor_tensor(
                        out=Xf[:, j, tsl(tt)], in0=ps[b][:, 0:TT], in1=Xf[:, j, tsl(tt)], op=ALU.add)),
                        reads=(bankB[b],), writes=(xB[j][tt],))
            return [b for r_ in mgB for b in r_]

        for k in range(KD):
            P.dma("sp", (lambda k=k: nc.sync.dma_start(out=Xf[:, k, :], in_=xin[p][:, k * T:(k + 1) * T])),
                  xk_sem[k], writes=tuple(xB[k]))
        if HALO:
            P.dma("sp", (lambda: nc.sync.dma_start(out=MASK[:, 0:1], in_=maskd)), mask_sem, writes=(maskB,))
        m_old = []
        for l in range(n_layers):
            m_old = ffn(l, 0, m_old, l == 0, (l, O_MN, True))
            m_old = mixer(l, m_old)
            nxt = (l + 1, O_F1N, True) if l + 1 < n_layers else (None, None, False)
            m_old = ffn(l, 1, m_old, True, nxt)
        outB = [newbuf() for _ in range(2)]
        handoff(allh, outB)
        for k in range(KD):
            r = k % 2
            P.op("dve", (lambda k=k, r=r: nc.vector.scalar_tensor_tensor(
                out=OUTR[r], in0=Xf[:, k, HALO:T], scalar=FIN[:, k:k + 1], in1=RSTD[:, HALO:T], op0=ALU.mult, op1=ALU.mult)),
                reads=tuple(xB[k]) + tuple(rstdB) + (constB,), writes=(outB[r],))
            P.dma("sp", (lambda k=k, r=r: nc.sync.dma_start(out=outd[p, k], in_=OUTR[r])), out_sem[r],
                  reads=(outB[r],))

    def barrier_tokens():
        toks = [(esem[e], P.cnt[e]) for e in Prog.ENG if P.cnt[e] > 0]
        for sm in ring_sem + xk_sem + out_sem + [bc_sem, mask_sem]:
            c = P.dcnt.get(id(sm), 0)
            if c:
                toks.append((sm, c))
        return toks

    barrier = []
    for p in range(n_passes):
        run_pass(p, barrier)
        barrier = barrier_tokens()
    for r in range(2):
        P.wait("sp", (out_sem[r], P.dcnt.get(id(out_sem[r]), 0)))

    with nc.Block() as block:
        @block.tensor
        def _(e):
            P.replay("pe", e)

        @block.scalar
        def _(e):
            P.replay("act", e)

        @block.vector
        def _(e):
            P.replay("dve", e)

        @block.gpsimd
        def _(e):
            P.replay("pool", e)

        @block.sync
        def _(e):
            P.replay("sp", e)
    es.close()
    return nc, P


def _pieces(w, kin):
    Lw, K, F = w.shape
    kc = K // 128
    a = w.reshape(Lw, kc, 128, F // 128, 128)
    a = np.transpose(a, (0, 3, 2, 1, 4))
    return np.ascontiguousarray(a).reshape(Lw, F // 128, 128, kc * 128)


def prep_shared(inp):
    f = lambda a: np.asarray(a, dtype=np.float32)
    sh = {}
    sh["f1w1"] = _pieces(f(inp["ffn1_w1"]), D)
    sh["f1w3"] = _pieces(f(inp["ffn1_w3"]), D)
    sh["f1w2"] = np.ascontiguousarray(f(inp["ffn1_w2"]))
    sh["f2w1"] = _pieces(f(inp["ffn2_w1"]), D)
    sh["f2w3"] = _pieces(f(inp["ffn2_w3"]), D)
    sh["f2w2"] = np.ascontiguousarray(f(inp["ffn2_w2"]))
    sh["win"] = _pieces(f(inp["w_in"]), D)
    wb = f(inp["w_branch"])
    sh["wbr"] = _pieces(wb.reshape(L * 3, 1024, D), 1024).reshape(L, 48, 128, 1024)
    sh["wout"] = _pieces(f(inp["w_out"]), D)

    def cols(v, n):
        return np.ascontiguousarray(v.reshape(n, 128).T)

    small = np.zeros((128, L * NS), np.float32)
    bc = np.zeros((L, 128, 4096), np.float32)
    for l in range(L):
        s = small[:, l * NS:(l + 1) * NS]
        s[:, O_F1N:O_F1N + 16] = cols(f(inp["ffn1_norm"])[l], 16)
        s[:, O_MN:O_MN + 16] = cols(f(inp["mix_norm"])[l], 16)
        s[:, O_F2N:O_F2N + 16] = cols(f(inp["ffn2_norm"])[l], 16)
        bg = f(inp["b_gate"])[l]
        for i in range(3):
            s[:, O_BG + i * 16:O_BG + (i + 1) * 16] = cols(bg[i], 16)
        scw = f(inp["sconv_w"])[l]
        s[:, O_SCW:O_SCW + 24] = np.transpose(scw.reshape(3, 8, 128), (2, 1, 0)).reshape(128, 24)
        ccw = f(inp["conf_conv_w"])[l]
        s[:, O_CCW:O_CCW + 248] = np.transpose(ccw.reshape(31, 8, 128), (2, 1, 0)).reshape(128, 248)
        s[:, O_CCB:O_CCB + 8] = cols(f(inp["conf_conv_b"])[l], 8)
        s[:, O_CLG:O_CLG + 8] = cols(f(inp["conf_ln_g"])[l], 8)
        s[:, O_CLB:O_CLB + 8] = cols(f(inp["conf_ln_b"])[l], 8)
        bc[l, :, 0:1024] = f(inp["gmlp_ln_g"])[l][None, :]
        bc[l, :, 1024:2048] = f(inp["gmlp_ln_b"])[l][None, :]
        bc[l, :, 2048:3072] = f(inp["gmlp_b_s"])[l].reshape(1, 1024)
        ws = f(inp["gmlp_w_s"])[l]
        bc[l, :, 3072:4096] = np.transpose(ws, (2, 0, 1)).reshape(128, 1024)
    sh["small"] = small
    sh["bc"] = bc
    sh["fin"] = cols(f(inp["final_norm"]), 16)
    sh["ident"] = np.eye(128, dtype=np.float32)
    return sh


def prep_x(x, c):
    b, s0 = c // 4, (c % 4) * 2 * TOUT
    res = {}
    for p, (T, NT, TT, HALO) in enumerate(GEO):
        start = s0 + p * TOUT - HALO
        buf = np.zeros((T, D), np.float32)
        lo = max(start, 0)
        buf[lo - start:] = x[b, lo:start + T]
        res[f"xin{p}"] = np.ascontiguousarray(np.transpose(buf.reshape(T, KD, 128), (2, 1, 0))).reshape(128, KD * T)
    res["mask"] = np.full((128, 1), 0.0 if s0 == 0 else 1.0, np.float32)
    return res


_CACHE = {}


def kernel(**inputs):
    x = np.asarray(inputs["x"], dtype=np.float32)
    sh = prep_shared(inputs)
    if "nc" not in _CACHE:
        _CACHE["nc"] = build_program(L, 2)[0]
    nc = _CACHE["nc"]
    in_maps = []
    for c in range(NCORES):
        m = dict(sh)
        m.update(prep_x(x, c))
        in_maps.append(m)
    res = run_bass_kernel_spmd(nc, in_maps, core_ids=list(range(NCORES)))
    out = np.empty((2, 8192, D), np.float32)
    for c in range(NCORES):
        o = np.asarray(res.results[c]["out"])
        b, s0 = c // 4, (c % 4) * 2 * TOUT
        for p in range(2):
            out[b, s0 + p * TOUT:s0 + (p + 1) * TOUT, :] = np.transpose(o[p], (2, 0, 1)).reshape(TOUT, D)
    return out
```
